# Optimizing a Trainium2 kernel written in Bass

```python
import jax
import jax.numpy as jnp
from jax import lax
import numpy as np

D_MODEL = 1024
BATCH = 4
SEQ = 4096
DEPTH = 4
DEC_BATCH = 128
DEC_SEQ = 1
PAST_LEN = 8192
PAGE_SIZE = 128

HEAD_DIM = 64
MIX_W = D_MODEL
MEM_HEADS = 4
MEM_W = MEM_HEADS * HEAD_DIM
N_MEM = 256
D_RNN = MIX_W - MEM_W
RNN_BLOCKS = D_RNN // HEAD_DIM
RNN_BLOCK = D_RNN // RNN_BLOCKS
CONV_A = 4
LRU_C = 8.0
N_Q = (MIX_W - MEM_W) // HEAD_DIM
N_KV = 4
GROUP = N_Q // N_KV
Q_W = N_Q * HEAD_DIM
KV_W = N_KV * HEAD_DIM
WINDOW = 128
ROPE_THETA = 10000.0
D_FF = 3 * D_MODEL
CONV_F = 3
N_A = DEPTH // 2
N_B = DEPTH - N_A
EPS = 1e-6
NEG = -1e30
ATT_SCALE = HEAD_DIM ** -0.5

kernel_name = 'yoco_hawk_swa_sink_convffn_decoder'


def _rmsnorm(x, g):
    x32 = x.astype(jnp.float32)
    r = lax.rsqrt(jnp.mean(x32 * x32, axis=-1, keepdims=True) + EPS)
    return (x32 * r * g.astype(jnp.float32)).astype(x.dtype)


def _rope(x, pos):
    half = HEAD_DIM // 2
    inv = ROPE_THETA ** (-jnp.arange(half, dtype=jnp.float32) / half)
    ang = pos.astype(jnp.float32)[:, None] * inv[None, :]
    cos = jnp.cos(ang)[:, None, :]
    sin = jnp.sin(ang)[:, None, :]
    x32 = x.astype(jnp.float32)
    x1, x2 = x32[..., :half], x32[..., half:]
    return jnp.concatenate([x1 * cos - x2 * sin, x2 * cos + x1 * sin], axis=-1).astype(x.dtype)


def _causal_dwconv(x, buf, w, b):
    k = w.shape[0]
    s = x.shape[1]
    xp = jnp.concatenate([buf.astype(x.dtype), x], axis=1)
    out = xp[:, 0:s] * w[0]
    for j in range(1, k):
        out = out + xp[:, j:j + s] * w[j]
    return out + b, xp[:, s:]


def _rglru(x, h0, w_gx, b_gx, w_ga, b_ga, lru_param):
    bsz, s, _ = x.shape
    x32 = x.astype(jnp.float32)
    xb = x32.reshape(bsz, s, RNN_BLOCKS, RNN_BLOCK)
    gx = jax.nn.sigmoid(jnp.einsum('bsnk,nkj->bsnj', xb, w_gx.astype(jnp.float32)).reshape(bsz, s, D_RNN) + b_gx.astype(jnp.float32))
    ga = jax.nn.sigmoid(jnp.einsum('bsnk,nkj->bsnj', xb, w_ga.astype(jnp.float32)).reshape(bsz, s, D_RNN) + b_ga.astype(jnp.float32))
    log_a = LRU_C * ga * jax.nn.log_sigmoid(lru_param.astype(jnp.float32))
    a = jnp.exp(log_a)
    bx = jnp.sqrt(-jnp.expm1(2.0 * log_a)) * gx * x32
    bx = bx.at[:, 0].add(a[:, 0] * h0.astype(jnp.float32))

    def comb(c1, c2):
        a1, b1 = c1
        a2, b2 = c2
        return a1 * a2, a2 * b1 + b2

    _, h = lax.associative_scan(comb, (a, bx), axis=1)
    return h.astype(x.dtype), h[:, -1].astype(x.dtype)


def _attend_sink(q, k, v, mask, sink):
    s = jnp.einsum('...tkgd,...lkd->...kgtl', q.astype(jnp.float32), k.astype(jnp.float32)) * ATT_SCALE
    s = jnp.where(mask, s, NEG)
    sk = sink.astype(jnp.float32)[..., None, None]
    m = jnp.maximum(jnp.max(s, axis=-1, keepdims=True), sk)
    p = jnp.exp(s - m)
    den = jnp.sum(p, axis=-1, keepdims=True) + jnp.exp(sk - m)
    o = jnp.einsum('...kgtl,...lkd->...tkgd', p / den, v.astype(jnp.float32))
    return o.astype(q.dtype)


def _swa_prompt(q, k, v, sink):
    bsz, s = q.shape[0], q.shape[1]
    nb = s // WINDOW
    qb = q.reshape(bsz, nb, WINDOW, N_KV, GROUP, HEAD_DIM)
    kb = k.reshape(bsz, nb, WINDOW, N_KV, HEAD_DIM)
    vb = v.reshape(bsz, nb, WINDOW, N_KV, HEAD_DIM)
    kk = jnp.concatenate([jnp.concatenate([jnp.zeros_like(kb[:, :1]), kb[:, :-1]], axis=1), kb], axis=2)
    vv = jnp.concatenate([jnp.concatenate([jnp.zeros_like(vb[:, :1]), vb[:, :-1]], axis=1), vb], axis=2)
    qi = jnp.arange(WINDOW)[:, None]
    kj = jnp.arange(2 * WINDOW)[None, :] - WINDOW
    band = (kj <= qi) & (kj >= qi - WINDOW)
    valid = (jnp.arange(nb)[:, None, None] > 0) | (kj >= 0)[None]
    mask = (band[None] & valid)[:, None, None]
    o = _attend_sink(qb, kk, vv, mask, sink.reshape(N_KV, GROUP))
    return o.reshape(bsz, s, Q_W)


def _swa_sample(q, kbuf, vbuf, knew, vnew, sink):
    dbsz, t = q.shape[0], q.shape[1]
    wb = kbuf.shape[1]
    k = jnp.concatenate([kbuf.astype(knew.dtype), knew], axis=1)
    v = jnp.concatenate([vbuf.astype(vnew.dtype), vnew], axis=1)
    qpos = PAST_LEN + jnp.arange(t, dtype=jnp.int32)
    kpos = jnp.concatenate([PAST_LEN - wb + jnp.arange(wb, dtype=jnp.int32), qpos])
    mask = (kpos[None, :] <= qpos[:, None]) & (kpos[None, :] >= qpos[:, None] - WINDOW)
    o = _attend_sink(q.reshape(dbsz, t, N_KV, GROUP, HEAD_DIM), k, v, mask, sink.reshape(N_KV, GROUP))
    return o.reshape(dbsz, t, Q_W)


def _mem_attn(q, k, v):
    s = jnp.einsum('bthd,bmhd->bhtm', q.astype(jnp.float32), k.astype(jnp.float32)) * ATT_SCALE
    p = jax.nn.softmax(s, axis=-1)
    o = jnp.einsum('bhtm,bmhd->bthd', p, v.astype(jnp.float32))
    return o.reshape(o.shape[0], o.shape[1], MEM_W).astype(q.dtype)


def _mem_kv(mem, g, w, kg):
    bsz = mem.shape[0]
    h = _rmsnorm(mem, g) @ w
    k = _rmsnorm(h[..., :MEM_W].reshape(bsz, N_MEM, MEM_HEADS, HEAD_DIM), kg)
    v = h[..., MEM_W:].reshape(bsz, N_MEM, MEM_HEADS, HEAD_DIM)
    return k, v


def _shared_kv(x, pos, g, w, kg):
    bsz, s = x.shape[0], x.shape[1]
    h = _rmsnorm(x, g) @ w
    k = _rope(_rmsnorm(h[..., :KV_W].reshape(bsz, s, N_KV, HEAD_DIM), kg), pos)
    v = h[..., KV_W:].reshape(bsz, s, N_KV, HEAD_DIM)
    return k, v


def _conv_ffn(x, buf, w_up, cw, cb, w_down):
    u = x @ w_up
    u, nbuf = _causal_dwconv(u, buf, cw, cb)
    return (jax.nn.gelu(u[..., :D_FF]) * u[..., D_FF:]) @ w_down, nbuf


def _trunk(x, pos, rnn_h0, rnn_conv0, ffn_conv0, mem_k, mem_v, kv_past, prm):
    bsz, s = x.shape[0], x.shape[1]
    hs, rcs, fcs = [], [], []
    k_sh = None
    v_sh = None
    for l in range(DEPTH):
        hn = _rmsnorm(x, prm['norm_mix_g'][l])
        if l < N_A:
            u = hn @ prm['w_in_a'][l]
            gate, xr, qm = u[..., :D_RNN], u[..., D_RNN:2 * D_RNN], u[..., 2 * D_RNN:]
            xr, rc = _causal_dwconv(xr, rnn_conv0[l], prm['rnn_conv_w'][l], prm['rnn_conv_b'][l])
            y, hl = _rglru(xr, rnn_h0[l], prm['w_gate_x'][l], prm['b_gate_x'][l], prm['w_gate_a'][l], prm['b_gate_a'][l], prm['lru_param'][l])
            main = y * jax.nn.gelu(gate)
            hs.append(hl)
            rcs.append(rc)
        else:
            j = l - N_A
            u = hn @ prm['w_in_b'][j]
            q, qm = u[..., :Q_W], u[..., Q_W:]
            q = _rope(_rmsnorm(q.reshape(bsz, s, N_Q, HEAD_DIM), prm['q_norm_g'][j]), pos)
            if kv_past is None:
                main = _swa_prompt(q, k_sh, v_sh, prm['sinks'][j])
            else:
                main = _swa_sample(q, kv_past[0], kv_past[1], k_sh, v_sh, prm['sinks'][j])
        qm = _rmsnorm(qm.reshape(bsz, s, MEM_HEADS, HEAD_DIM), prm['mem_q_norm_g'][l])
        mo = _mem_attn(qm, mem_k[l], mem_v[l])
        x = x + jnp.concatenate([main, mo], axis=-1) @ prm['w_out'][l]
        hn = _rmsnorm(x, prm['norm_ffn_g'][l])
        f, fc = _conv_ffn(hn, ffn_conv0[l], prm['w_ffn_up'][l], prm['ffn_conv_w'][l], prm['ffn_conv_b'][l], prm['w_ffn_down'][l])
        x = x + f
        fcs.append(fc)
        if l == N_A - 1:
            k_sh, v_sh = _shared_kv(x, pos, prm['kv_norm_g'], prm['w_kv'], prm['k_norm_g'])
    if kv_past is None:
        keep = min(WINDOW, s)
        k_state, v_state = k_sh[:, s - keep:], v_sh[:, s - keep:]
    else:
        k_state, v_state = k_sh, v_sh
    return x, jnp.stack(hs), jnp.stack(rcs), jnp.stack(fcs), k_state, v_state


def setup_inputs(seed: int = 0) -> dict:
    key = jax.random.key(seed)
    ks = jax.random.split(key, 36)
    f32 = jnp.float32

    def nrm(i, shape, scale):
        return jax.random.normal(ks[i], shape, f32) * scale

    def gain(i, shape):
        return 1.0 + nrm(i, shape, 0.1)

    w_buf = min(WINDOW, PAST_LEN)
    lru_a = jax.random.uniform(ks[19], (N_A, D_RNN), f32, 0.9, 0.999)
    out_scale = MIX_W ** -0.5 * (2 * DEPTH) ** -0.5
    return {
        'x_prompt': nrm(0, (BATCH, SEQ, D_MODEL), 1.0),
        'x_sample': nrm(1, (DEC_BATCH, DEC_SEQ, D_MODEL), 1.0),
        'state_rglru_h': nrm(2, (N_A, DEC_BATCH, D_RNN), 0.5),
        'state_rglru_conv': nrm(3, (N_A, DEC_BATCH, CONV_A - 1, D_RNN), 0.5),
        'state_ffn_conv': nrm(4, (DEPTH, DEC_BATCH, CONV_F - 1, 2 * D_FF), 0.5),
        'cache_swa_k': nrm(5, (DEC_BATCH, w_buf, N_KV, HEAD_DIM), 1.0),
        'cache_swa_v': nrm(6, (DEC_BATCH, w_buf, N_KV, HEAD_DIM), 1.0),
        'cache_mem_k': nrm(7, (DEPTH, DEC_BATCH, N_MEM, MEM_HEADS, HEAD_DIM), 1.0),
        'cache_mem_v': nrm(8, (DEPTH, DEC_BATCH, N_MEM, MEM_HEADS, HEAD_DIM), 1.0),
        'mem_prompt': nrm(9, (BATCH, N_MEM, D_MODEL), 1.0),
        'norm_mix_g': gain(10, (DEPTH, D_MODEL)),
        'norm_ffn_g': gain(11, (DEPTH, D_MODEL)),
        'w_in_a': nrm(12, (N_A, D_MODEL, 2 * D_RNN + MEM_W), D_MODEL ** -0.5),
        'rnn_conv_w': nrm(13, (N_A, CONV_A, D_RNN), CONV_A ** -0.5),
        'rnn_conv_b': nrm(14, (N_A, D_RNN), 0.02),
        'w_gate_x': nrm(15, (N_A, RNN_BLOCKS, RNN_BLOCK, RNN_BLOCK), RNN_BLOCK ** -0.5),
        'b_gate_x': nrm(16, (N_A, D_RNN), 0.02),
        'w_gate_a': nrm(17, (N_A, RNN_BLOCKS, RNN_BLOCK, RNN_BLOCK), RNN_BLOCK ** -0.5),
        'b_gate_a': nrm(18, (N_A, D_RNN), 0.02),
        'lru_param': jnp.log(lru_a) - jnp.log1p(-lru_a),
        'w_in_b': nrm(20, (N_B, D_MODEL, Q_W + MEM_W), D_MODEL ** -0.5),
        'q_norm_g': gain(21, (N_B, HEAD_DIM)),
        'sinks': nrm(22, (N_B, N_Q), 0.5),
        'kv_norm_g': gain(23, (D_MODEL,)),
        'w_kv': nrm(24, (D_MODEL, 2 * KV_W), D_MODEL ** -0.5),
        'k_norm_g': gain(25, (HEAD_DIM,)),
        'mem_norm_g': gain(26, (DEPTH, D_MODEL)),
        'w_mem_kv': nrm(27, (DEPTH, D_MODEL, 2 * MEM_W), D_MODEL ** -0.5),
        'mem_q_norm_g': gain(28, (DEPTH, HEAD_DIM)),
        'mem_k_norm_g': gain(29, (DEPTH, HEAD_DIM)),
        'w_out': nrm(30, (DEPTH, MIX_W, D_MODEL), out_scale),
        'w_ffn_up': nrm(31, (DEPTH, D_MODEL, 2 * D_FF), D_MODEL ** -0.5),
        'ffn_conv_w': nrm(32, (DEPTH, CONV_F, 2 * D_FF), CONV_F ** -0.5),
        'ffn_conv_b': nrm(33, (DEPTH, 2 * D_FF), 0.02),
        'w_ffn_down': nrm(34, (DEPTH, D_FF, D_MODEL), D_FF ** -0.5 * (2 * DEPTH) ** -0.5),
    }


def reference(x_prompt, x_sample, state_rglru_h, state_rglru_conv, state_ffn_conv, cache_swa_k, cache_swa_v, cache_mem_k, cache_mem_v, mem_prompt, norm_mix_g, norm_ffn_g, w_in_a, rnn_conv_w, rnn_conv_b, w_gate_x, b_gate_x, w_gate_a, b_gate_a, lru_param, w_in_b, q_norm_g, sinks, kv_norm_g, w_kv, k_norm_g, mem_norm_g, w_mem_kv, mem_q_norm_g, mem_k_norm_g, w_out, w_ffn_up, ffn_conv_w, ffn_conv_b, w_ffn_down):
    prm = {
        'norm_mix_g': norm_mix_g, 'norm_ffn_g': norm_ffn_g, 'w_in_a': w_in_a,
        'rnn_conv_w': rnn_conv_w, 'rnn_conv_b': rnn_conv_b, 'w_gate_x': w_gate_x,
        'b_gate_x': b_gate_x, 'w_gate_a': w_gate_a, 'b_gate_a': b_gate_a,
        'lru_param': lru_param, 'w_in_b': w_in_b, 'q_norm_g': q_norm_g, 'sinks': sinks,
        'kv_norm_g': kv_norm_g, 'w_kv': w_kv, 'k_norm_g': k_norm_g,
        'mem_q_norm_g': mem_q_norm_g, 'w_out': w_out, 'w_ffn_up': w_ffn_up,
        'ffn_conv_w': ffn_conv_w, 'ffn_conv_b': ffn_conv_b, 'w_ffn_down': w_ffn_down,
    }
    mks, mvs = [], []
    for l in range(DEPTH):
        mk, mv = _mem_kv(mem_prompt, mem_norm_g[l], w_mem_kv[l], mem_k_norm_g[l])
        mks.append(mk)
        mvs.append(mv)
    p_mem_k = jnp.stack(mks)
    p_mem_v = jnp.stack(mvs)
    bsz, s = x_prompt.shape[0], x_prompt.shape[1]
    dt = x_prompt.dtype
    h0 = jnp.zeros((N_A, bsz, D_RNN), dt)
    rc0 = jnp.zeros((N_A, bsz, CONV_A - 1, D_RNN), dt)
    fc0 = jnp.zeros((DEPTH, bsz, CONV_F - 1, 2 * D_FF), dt)
    pos_p = jnp.arange(s, dtype=jnp.int32)
    y_prompt, p_h, p_rc, p_fc, p_k, p_v = _trunk(x_prompt, pos_p, h0, rc0, fc0, p_mem_k, p_mem_v, None, prm)
    pos_s = PAST_LEN + jnp.arange(x_sample.shape[1], dtype=jnp.int32)
    y_sample, s_h, s_rc, s_fc, s_k, s_v = _trunk(x_sample, pos_s, state_rglru_h, state_rglru_conv, state_ffn_conv, cache_mem_k, cache_mem_v, (cache_swa_k, cache_swa_v), prm)
    return (y_prompt, y_sample, p_h, p_rc, p_fc, p_k, p_v, p_mem_k, p_mem_v, s_h, s_rc, s_fc, s_k, s_v)
```

```python
import contextlib
import os
import numpy as np
import ml_dtypes
import concourse.bass as bass
import concourse.mybir as mybir
from concourse.bass_utils import run_bass_kernel_spmd

F32 = mybir.dt.float32
BF16 = mybir.dt.bfloat16
AF = mybir.ActivationFunctionType
ALU = mybir.AluOpType
AX = mybir.AxisListType

D = 1024
DEPTH = 4
HD = 64
DRNN = 768
DFF = 3072
NMEM = 256
PAST = 8192
EPS = 1e-6
NS = 16
TC = 512
PIECE = 8192
SEM_ROT = 15000


class Tile:
    __slots__ = ("t", "w", "r", "dsem", "dcnt", "dw", "dr", "name")

    def __init__(self, t, name):
        self.t = t
        self.name = name
        self.w = {}
        self.r = {}
        self.dsem = {}
        self.dcnt = {}
        self.dw = {}
        self.dr = {}

    def __getitem__(self, k):
        return self.t[k]


class Eng:
    def __init__(self, name, obj):
        self.name = name
        self.obj = obj
        self.sems = []
        self.cnt = 0
        self.seen = {}
        self.seen_d = {}


class KB:
    def __init__(self, nc, es):
        self.nc = nc
        self.es = es
        self.E = {
            "pe": Eng("pe", nc.tensor), "act": Eng("act", nc.scalar), "dve": Eng("dve", nc.vector),
            "pool": Eng("pool", nc.gpsimd), "sp": Eng("sp", nc.sync),
        }
        self.nsem = 0
        self.tiles = []
        self.pools = {}
        self.psums = []
        self.psi = 0

    def sem(self, name):
        self.nsem += 1
        return self.es.enter_context(self.nc.semaphore(f"{name}_{self.nsem}"))

    def sb(self, name, shape, dtype, es=None):
        t = (es or self.es).enter_context(self.nc.sbuf_tensor("sb_" + name, list(shape), dtype))
        tl = Tile(t, name)
        self.tiles.append(tl)
        return tl

    def pseudo(self, name):
        tl = Tile(None, name)
        self.tiles.append(tl)
        return tl

    def init_psum(self):
        for i in range(8):
            t = self.es.enter_context(self.nc.psum_tensor(f"ps{i}", [128, 512], F32))
            self.psums.append(Tile(t, f"ps{i}"))

    def ps(self):
        t = self.psums[self.psi % 8]
        self.psi += 1
        return t

    def pool(self, name, n, shape, dtype, es=None):
        key = name
        if key not in self.pools:
            self.pools[key] = [[self.sb(f"{name}{i}", shape, dtype, es) for i in range(n)], 0]
        p = self.pools[key]
        t = p[0][p[1] % len(p[0])]
        p[1] += 1
        return t

    def _cur_sem(self, e):
        ep = e.cnt // SEM_ROT
        while len(e.sems) <= ep:
            e.sems.append(self.sem(e.name))
        return ep

    def _wait_eng(self, e, oname, val):
        if e.seen.get(oname, 0) >= val:
            return
        o = self.E[oname]
        ep = (val - 1) // SEM_ROT
        e.obj.wait_ge(o.sems[ep], val - ep * SEM_ROT)
        e.seen[oname] = val

    def _wait_dma(self, e, tile, q, val):
        if val <= 0:
            return
        k = (id(tile), q)
        if e.seen_d.get(k, 0) >= val:
            return
        e.obj.wait_ge(tile.dsem[q], val)
        e.seen_d[k] = val

    def _deps(self, e, reads, writes):
        for t in reads:
            for on, v in t.w.items():
                if not (on == "pe" and e.name == "pe"):
                    self._wait_eng(e, on, v)
            for q, v in t.dw.items():
                self._wait_dma(e, t, q, v)
        for t in writes:
            for on, v in t.w.items():
                if not (on == "pe" and e.name == "pe"):
                    self._wait_eng(e, on, v)
            for on, v in t.r.items():
                if not (on == "pe" and e.name == "pe"):
                    self._wait_eng(e, on, v)
            for q, v in t.dw.items():
                self._wait_dma(e, t, q, v)
            for q, v in t.dr.items():
                self._wait_dma(e, t, q, v)

    def op(self, en, fn, reads=(), writes=()):
        e = self.E[en]
        self._deps(e, reads, writes)
        ep = self._cur_sem(e)
        ins = fn()
        ins.then_inc(e.sems[ep], 1)
        e.cnt += 1
        e.seen[en] = max(e.seen.get(en, 0), 0)
        for t in writes:
            t.w[en] = e.cnt
        for t in reads:
            t.r[en] = e.cnt

    def dma(self, qn, out, in_, reads=(), writes=(), semtile=None, extra=()):
        e = self.E[qn]
        self._deps(e, reads, writes)
        for (tl, q, v) in extra:
            self._wait_dma(e, tl, q, v)
        st = semtile or (writes[0] if writes else reads[0])
        if qn not in st.dsem:
            st.dsem[qn] = self.sem("d")
            st.dcnt[qn] = 0
        ins = e.obj.dma_start(out=out, in_=in_)
        ins.then_inc(st.dsem[qn], 16)
        st.dcnt[qn] += 16
        assert st.dcnt[qn] < 30000, st.name
        for t in writes:
            assert t is st
            t.dw[qn] = st.dcnt[qn]
        for t in reads:
            assert t is st
            t.dr[qn] = st.dcnt[qn]
        return (st, qn, st.dcnt[qn])

    def barrier(self):
        for e in self.E.values():
            for on, o in self.E.items():
                if o.cnt > 0 and on != e.name:
                    self._wait_eng(e, on, o.cnt)
            for t in self.tiles:
                for q, v in t.dcnt.items():
                    self._wait_dma(e, t, q, v)

    def finish(self):
        e = self.E["sp"]
        for on, o in self.E.items():
            if o.cnt > 0 and on != "sp":
                self._wait_eng(e, on, o.cnt)
        for t in self.tiles:
            for q, v in t.dcnt.items():
                self._wait_dma(e, t, q, v)


def _kpack(w):
    K, C = w.shape
    return np.ascontiguousarray(w.reshape(K // 128, 128, C).transpose(1, 0, 2).reshape(128, -1))


def _pad_piece(a):
    out = np.zeros((128, PIECE), np.float32)
    out[:, : a.shape[1]] = a
    return out


def piece_names():
    names = []
    for l in range(2):
        names += [f"g{l}", f"ina{l}_0", f"ina{l}_1", f"out{l}"] + [f"up{l}_{i}" for i in range(6)] + [f"dn{l}_{i}" for i in range(3)]
    names += ["kv"]
    for l in range(2, 4):
        names += [f"inb{l}", f"out{l}"] + [f"up{l}_{i}" for i in range(6)] + [f"dn{l}_{i}" for i in range(3)]
    return names


def make_wpack(inp):
    P = {}
    for l in range(2):
        bd = np.zeros((128, 2 * 768), np.float32)
        for gi, nm in enumerate(["w_gate_x", "w_gate_a"]):
            w = inp[nm][l]
            for c in range(6):
                for i in range(2):
                    bd[64 * i:64 * i + 64, gi * 768 + c * 128 + 64 * i: gi * 768 + c * 128 + 64 * i + 64] = w[2 * c + i]
        P[f"g{l}"] = bd
        wi = inp["w_in_a"][l]
        P[f"ina{l}_0"] = _kpack(wi[:, :896])
        P[f"ina{l}_1"] = _kpack(wi[:, 896:])
    for l in range(4):
        P[f"out{l}"] = _kpack(inp["w_out"][l])
        wu = inp["w_ffn_up"][l]
        for i in range(6):
            P[f"up{l}_{i}"] = _kpack(np.concatenate([wu[:, i * 512:(i + 1) * 512], wu[:, DFF + i * 512: DFF + (i + 1) * 512]], axis=1))
        wd = inp["w_ffn_down"][l]
        for i in range(3):
            P[f"dn{l}_{i}"] = _kpack(wd[i * 1024:(i + 1) * 1024])
    for l in range(2, 4):
        P[f"inb{l}"] = _kpack(inp["w_in_b"][l - 2])
    wkv = inp["w_kv"]
    cols = []
    for kv in range(4):
        cols += [wkv[:, kv * 64:(kv + 1) * 64], wkv[:, kv * 64:(kv + 1) * 64]]
    cols.append(wkv[:, 256:])
    P["kv"] = _kpack(np.concatenate(cols, axis=1))
    names = piece_names()
    return np.stack([_pad_piece(P[n]) for n in names]), {n: i for i, n in enumerate(names)}


def _fm(v):
    return np.ascontiguousarray(v.reshape(-1, 128).T)


def make_cpk(inp):
    cols = []
    idx = {}

    def add(name, a):
        a = np.asarray(a, np.float32)
        if a.ndim == 1:
            a = a[:, None]
        idx[name] = sum(c.shape[1] for c in cols)
        cols.append(a)

    def head_vec(g):
        return np.concatenate([g, g])[:, None]

    for l in range(4):
        add(f"gmix{l}", _fm(inp["norm_mix_g"][l]))
        add(f"gffn{l}", _fm(inp["norm_ffn_g"][l]))
        for j in range(3):
            add(f"fcw{l}_{j}", _fm(inp["ffn_conv_w"][l, j]))
        add(f"fcb{l}", _fm(inp["ffn_conv_b"][l]))
        add(f"mqg{l}", head_vec(inp["mem_q_norm_g"][l]))
        add(f"memg{l}", _fm(inp["mem_norm_g"][l]))
    for l in range(2):
        for j in range(4):
            add(f"rcw{l}_{j}", _fm(inp["rnn_conv_w"][l, j]))
        add(f"rcb{l}", _fm(inp["rnn_conv_b"][l]))
        add(f"bgx{l}", _fm(inp["b_gate_x"][l]))
        add(f"bga{l}", _fm(inp["b_gate_a"][l]))
        add(f"lru{l}", _fm(inp["lru_param"][l]))
    for j in range(2):
        add(f"qg{j + 2}", head_vec(inp["q_norm_g"][j]))
    add("kvg", _fm(inp["kv_norm_g"]))
    add("kng", head_vec(inp["k_norm_g"]))
    return np.ascontiguousarray(np.concatenate(cols, axis=1)), idx


def rope_tables(pos):
    half = HD // 2
    inv = (10000.0 ** (-np.arange(half, dtype=np.float32) / half)).astype(np.float32)
    ang = pos.astype(np.float32)[None, :] * inv[:, None]
    c = np.cos(ang).astype(np.float32)
    s = np.sin(ang).astype(np.float32)
    return np.ascontiguousarray(np.tile(c, (4, 1))), np.ascontiguousarray(np.tile(s, (4, 1)))


def const_mats():
    ident = np.eye(128, dtype=np.float32)
    rot = np.zeros((128, 128), np.float32)
    for b in range(2):
        for d in range(32):
            rot[64 * b + d + 32, 64 * b + d] = -1.0
            rot[64 * b + d, 64 * b + d + 32] = 1.0
    ones = np.ones((128, 128), np.float32)
    blk = np.zeros((128, 128), np.float32)
    blk[:64, :64] = 1.0
    blk[64:, 64:] = 1.0
    l = np.arange(128)[:, None]
    t = np.arange(128)[None, :]
    tri_cur = (l <= t).astype(np.float32)
    tri_prev = (l >= t).astype(np.float32)
    mask = np.concatenate([tri_cur, tri_prev], axis=1)
    sel = (np.arange(128)[:, None] // 8 == np.arange(16)[None, :]).astype(np.float32)
    bf = ml_dtypes.bfloat16
    return dict(ident=ident, rotm=rot, ones_bf=ones.astype(bf), blk_bf=blk.astype(bf), mask_bf=mask.astype(bf), sel=sel)


def build(SEQ):
    NCH = SEQ // TC
    names = piece_names()
    pidx = {n: i for i, n in enumerate(names)}
    NP = len(names)
    nc = bass.Bass("TRN2", target_bir_lowering=False)

    def din(name, shape, dt=F32):
        return nc.dram_tensor(name, list(shape), dt, kind="ExternalInput").ap()

    def dout(name, shape):
        return nc.dram_tensor(name, list(shape), F32, kind="ExternalOutput").ap()

    xp = din("xp", [SEQ, D]); xs = din("xs", [NS, D]); memp = din("memp", [NMEM, D])
    st_h = din("st_h", [2, NS, DRNN]); st_rc = din("st_rc", [2, NS, 3 * DRNN]); st_fc = din("st_fc", [4, NS, 2 * 2 * DFF])
    c_sk = din("c_sk", [NS * 8, 16 * 256]); c_sv = din("c_sv", [NS * 8, 16 * 256])
    c_mk = din("c_mk", [4, NS * 8, 32 * 256]); c_mv = din("c_mv", [4, NS * 8, 32 * 256])
    wpack = din("wpack", [NP, 128, PIECE]); wmem = din("wmem", [4, 128, 8 * 512])
    cpk_np_cols = None
    cpk_d = din("cpk", [128, 1000])
    rcos = din("rcos", [128, SEQ]); rsin = din("rsin", [128, SEQ])
    rcs_s = din("rcs_s", [128, 2])
    rowc = din("rowc", [4 * 256 + 24])
    ident_d = din("ident", [128, 128]); rotm_d = din("rotm", [128, 128]); sel_d = din("sel", [128, 16])
    ones_d = din("ones_bf", [128, 128], BF16); blk_d = din("blk_bf", [128, 128], BF16); mask_d = din("mask_bf", [128, 256], BF16)
    y_p = dout("y_p", [SEQ, D]); y_s = dout("y_s", [NS, D])
    o_ph = dout("o_ph", [2, DRNN]); o_prc = dout("o_prc", [2, 3, DRNN]); o_pfc = dout("o_pfc", [4, 2, 2 * DFF])
    o_pk = dout("o_pk", [128, 256]); o_pv = dout("o_pv", [128, 256])
    o_pmk = dout("o_pmk", [4, NMEM, 256]); o_pmv = dout("o_pmv", [4, NMEM, 256])
    o_sh = dout("o_sh", [2, NS, DRNN]); o_src = dout("o_src", [2, NS, 3 * DRNN]); o_sfc = dout("o_sfc", [4, NS, 2 * 2 * DFF])
    o_sk = dout("o_sk", [NS, 256]); o_sv = dout("o_sv", [NS, 256])
    wbf = nc.dram_tensor("wbf", [NP, 128, PIECE], BF16, kind="Internal").ap()

    with contextlib.ExitStack() as es:
        K = KB(nc, es)
        K.init_psum()
        op = K.op
        CI = build.cidx

        slots = [K.sb(f"slot{i}", [128, PIECE], BF16) for i in range(4)]
        cpk = K.sb("cpk", [128, 1000], F32)
        ident = K.sb("ident", [128, 128], F32); rotm = K.sb("rotm", [128, 128], F32); sel = K.sb("sel", [128, 16], F32)
        ones_bf = K.sb("ones_bf", [128, 128], BF16); blk_bf = K.sb("blk_bf", [128, 128], BF16); mask_bf = K.sb("mask_bf", [128, 256], BF16)
        rowc_t = K.sb("rowc", [128, 24], F32)
        esink = K.sb("esink", [128, 24], F32)
        clt = K.sb("clt", [128, 12], F32); hclt = K.sb("hclt", [128, 12], F32)
        hbx = K.sb("hbx", [128, 12], F32); hba = K.sb("hba", [128, 12], F32); lnhalf = K.sb("lnhalf", [128, 1], F32)
        rcs_t = K.sb("rcs_s", [128, 2], F32)
        for tl, src in [(cpk, cpk_d), (ident, ident_d), (rotm, rotm_d), (sel, sel_d), (ones_bf, ones_d), (blk_bf, blk_d), (mask_bf, mask_d), (rcs_t, rcs_s)]:
            K.dma("sp", tl[:], src[:, :], writes=[tl])
        K.dma("sp", rowc_t[:], rowc[1024:1048].partition_broadcast(128), writes=[rowc_t])

        def cc(name, j=0, n=1):
            b = CI[name] + j
            return cpk[:, b:b + n]

        op("act", lambda: nc.scalar.activation(out=esink[:], in_=rowc_t[:, 0:24], func=AF.Exp), [rowc_t], [esink])
        for l in range(2):
            tmp = K.pool("f", 8, [128, 516], F32)
            op("act", lambda: nc.scalar.activation(out=tmp[:, 0:6], in_=cc(f"lru{l}", 0, 6), func=AF.Exp, scale=-1.0), [cpk], [tmp])
            op("act", lambda: nc.scalar.activation(out=tmp[:, 8:14], in_=tmp[:, 0:6], func=AF.Ln, bias=1.0), [tmp], [tmp])
            op("pool", lambda: nc.gpsimd.tensor_scalar(out=clt[:, l * 6:l * 6 + 6], in0=tmp[:, 8:14], scalar1=-8.0, scalar2=None, op0=ALU.mult), [tmp], [clt])
            op("pool", lambda: nc.gpsimd.tensor_scalar(out=hclt[:, l * 6:l * 6 + 6], in0=tmp[:, 8:14], scalar1=-4.0, scalar2=None, op0=ALU.mult), [tmp], [hclt])
            op("pool", lambda: nc.gpsimd.tensor_scalar(out=hbx[:, l * 6:l * 6 + 6], in0=cc(f"bgx{l}", 0, 6), scalar1=0.5, scalar2=None, op0=ALU.mult), [cpk], [hbx])
            op("pool", lambda: nc.gpsimd.tensor_scalar(out=hba[:, l * 6:l * 6 + 6], in0=cc(f"bga{l}", 0, 6), scalar1=0.5, scalar2=None, op0=ALU.mult), [cpk], [hba])
        op("pool", lambda: nc.gpsimd.memset(lnhalf[:, :], -0.6931471805599453), [], [lnhalf])

        K.pool("f", 8, [128, 516], F32); K.pool("b", 12, [128, 512], BF16); K.pool("pm", 12, [128, 256], BF16); K.pool("fa", 4, [128, 516], F32)
        for nm in ("f", "b", "pm", "fa"):
            K.pools[nm][1] = 0
        WS = {"n": 0, "loaded": 0, "store": {}, "done": 0}
        stream = []

        def ws_issue(upto):
            while WS["loaded"] < min(upto, len(stream)):
                g = WS["loaded"]
                pid, p0 = stream[g]
                sl = slots[g % 4]
                if p0:
                    K.dma("pool", sl[:], wpack[pid], writes=[sl])
                    WS["store"][pid] = K.dma("sp", wbf[pid], sl[:], reads=[sl])
                else:
                    K.dma("sp", sl[:], wbf[pid], writes=[sl], extra=[WS["store"][pid]])
                WS["loaded"] += 1

        def ws_get(name, hold=False):
            g = WS["n"]
            assert stream[g][0] == pidx[name], (name, names[stream[g][0]])
            if not hold:
                WS["done"] = g
            ws_issue(min(g + 4, WS["done"] + 4))
            assert WS["loaded"] > g
            WS["n"] += 1
            return slots[g % 4]

        def add_pass(p0):
            for n in names:
                stream.append((pidx[n], p0))

        def fill(n, N=512):
            return

        def interleave2(ga, wa, gb, wb):
            ga = list(ga); gb = list(gb)
            aa = []; ab = []
            while ga or gb or aa or ab:
                while ga and len(aa) < wa:
                    aa.append(ga.pop(0))
                while gb and len(ab) < wb:
                    ab.append(gb.pop(0))
                for lst in (aa, ab):
                    for g in list(lst):
                        try:
                            next(g)
                        except StopIteration:
                            lst.remove(g)

        def interleave(gens, width=2):
            active = []
            gens = list(gens)
            while gens or active:
                while gens and len(active) < width:
                    active.append(gens.pop(0))
                for g in list(active):
                    try:
                        next(g)
                    except StopIteration:
                        active.remove(g)

        def fp():
            return K.pool("f", 8, [128, 516], F32)

        def bp():
            return K.pool("b", 12, [128, 512], BF16)

        def rmsnorm(X, gname, out, N):
            ss = K.ps()
            for c in range(8):
                sq = bp()
                if c % 2 == 0 or N != TC:
                    op("act", lambda: nc.scalar.activation(out=sq[:, :N], in_=X[c][:, :N], func=AF.Square), [X[c]], [sq])
                else:
                    op("pool", lambda: nc.gpsimd.tensor_tensor(out=sq[:, :N], in0=X[c][:, :N], in1=X[c][:, :N], op=ALU.mult), [X[c]], [sq])
                op("pe", lambda: nc.tensor.matmul(ss[:, :N], ones_bf[:], sq[:, :N], start=(c == 0), stop=(c == 7)), [sq, ones_bf], [ss])
            fill(20, N)
            rb = fp()
            op("act", lambda: nc.scalar.activation(out=rb[:, :N], in_=ss[:, :N], func=AF.Ln, scale=1.0 / D, bias=EPS), [ss], [rb])
            op("act", lambda: nc.scalar.activation(out=rb[:, :N], in_=rb[:, :N], func=AF.Exp, scale=-0.5), [rb], [rb])
            for c in range(8):
                op("dve", lambda: nc.vector.scalar_tensor_tensor(out=out[c][:, :N], in0=X[c][:, :N], scalar=cc(gname, c), in1=rb[:, :N], op0=ALU.mult, op1=ALU.mult), [X[c], rb, cpk], [out[c]])

        def proj(slot, stride, col, hn, N):
            pt = K.ps()
            for k in range(8):
                op("pe", lambda: nc.tensor.matmul(pt[:, :N], slot[:, k * stride + col:k * stride + col + 128], hn[k][:, :N], start=(k == 0), stop=(k == 7)), [slot, hn[k]], [pt])
            return pt

        def headnorm(pq, gcol, N, out_t, out_dt_bf16):
            for _ in headnorm_g(pq, gcol, N, out_t):
                pass

        def headnorm_g(pq, gcol, N, out_t):
            sq = bp()
            op("act", lambda: nc.scalar.activation(out=sq[:, :N], in_=pq[:, :N], func=AF.Square), [pq], [sq])
            yield
            fill(4, N)
            ssh = K.ps()
            op("pe", lambda: nc.tensor.matmul(ssh[:, :N], blk_bf[:], sq[:, :N], start=True, stop=True), [sq, blk_bf], [ssh])
            r = fp()
            op("act", lambda: nc.scalar.activation(out=r[:, :N], in_=ssh[:, :N], func=AF.Ln, scale=1.0 / HD, bias=EPS), [ssh], [r])
            op("act", lambda: nc.scalar.activation(out=r[:, :N], in_=r[:, :N], func=AF.Exp, scale=-0.5), [r], [r])
            op("dve", lambda: nc.vector.scalar_tensor_tensor(out=out_t[:, :N], in0=pq[:, :N], scalar=gcol, in1=r[:, :N], op0=ALU.mult, op1=ALU.mult), [pq, r, cpk], [out_t])
            yield

        def rope(qn, cos_ap, sin_ap, cs_tiles, out_ap, out_tile, N, per_part):
            fill(5, N)
            pr = K.ps()
            op("pe", lambda: nc.tensor.matmul(pr[:, :N], rotm[:], qn[:, :N], start=True, stop=True), [rotm, qn], [pr])
            t1 = fp(); t2 = fp()
            if per_part:
                op("dve", lambda: nc.vector.tensor_scalar(out=t1[:, :N], in0=qn[:, :N], scalar1=cos_ap, scalar2=None, op0=ALU.mult), [qn] + cs_tiles, [t1])
                op("dve", lambda: nc.vector.tensor_scalar(out=t2[:, :N], in0=pr[:, :N], scalar1=sin_ap, scalar2=None, op0=ALU.mult), [pr] + cs_tiles, [t2])
            else:
                op("dve", lambda: nc.vector.tensor_tensor(out=t1[:, :N], in0=qn[:, :N], in1=cos_ap, op=ALU.mult), [qn] + cs_tiles, [t1])
                op("dve", lambda: nc.vector.tensor_tensor(out=t2[:, :N], in0=pr[:, :N], in1=sin_ap, op=ALU.mult), [pr] + cs_tiles, [t2])
            rope.cnt = getattr(rope, "cnt", 0) + 1
            if rope.cnt % 2 == 0 or N != TC:
                op("pool", lambda: nc.gpsimd.tensor_tensor(out=out_ap, in0=t1[:, :N], in1=t2[:, :N], op=ALU.add), [t1, t2], [out_tile])
            else:
                op("dve", lambda: nc.vector.tensor_tensor(out=out_ap, in0=t1[:, :N], in1=t2[:, :N], op=ALU.add), [t1, t2], [out_tile])
            return t1, t2

        def transpose_to(pt, col, in_ap, in_tile, kparts):
            op("pe", lambda: nc.tensor.transpose(out=pt[:in_ap.shape[1], col:col + kparts], in_=in_ap, identity=ident[0:kparts, 0:kparts]), [in_tile, ident], [pt])

        def ffn(l, X, hn, act_t, N, sample, S):
            KFFN = int(os.environ.get("KFFN", "99")) if (l == 1 and not sample) else 99
            for i in range(6):
                if i >= KFFN:
                    return
                slot = ws_get(f"up{l}_{i}")
                for jj in range(4):
                    j = 4 * i + jj
                    accs = []
                    for half in range(2):
                        cidx = j + 24 * half
                        pu = proj(slot, 1024, half * 512 + jj * 128, hn, N)
                        acc = fp()
                        op("act", lambda: nc.scalar.activation(out=acc[:, :N], in_=pu[:, :N], func=AF.Identity, scale=cc(f"fcw{l}_2", cidx), bias=cc(f"fcb{l}", cidx)), [pu, cpk], [acc])
                        if not sample:
                            hv = S["HIST"][l][:].rearrange("p (j c) -> p j c", c=48)[:, :, cidx]
                            ub = fp()
                            op("pool", lambda: nc.gpsimd.tensor_copy(out=ub[:, 0:2], in_=hv), [S["HIST"][l]], [ub])
                            op("act", lambda: nc.scalar.copy(out=ub[:, 2:2 + N], in_=pu[:, :N]), [pu], [ub])
                            op("pool", lambda: nc.gpsimd.tensor_copy(out=hv, in_=ub[:, N:N + 2]), [ub], [S["HIST"][l]])
                            taps = [(ub[:, 1:1 + N], [ub]), (ub[:, 0:N], [ub])]
                        else:
                            fst = S["FST"]
                            fv = fst[:].rearrange("p (j c s) -> p j c s", j=2, c=48)
                            op("act", lambda: nc.scalar.copy(out=S["UNEW"][:].rearrange("p (c s) -> p c s", c=48)[:, cidx, :], in_=pu[:, :N]), [pu], [S["UNEW"]])
                            taps = [(fv[:, 1, cidx, :], [fst]), (fv[:, 0, cidx, :], [fst])]
                        for ti, (tap, trd) in enumerate(taps):
                            wname = f"fcw{l}_{1 - ti}"
                            op("dve", lambda: nc.vector.scalar_tensor_tensor(out=acc[:, :N], in0=tap, scalar=cc(wname, cidx), in1=acc[:, :N], op0=ALU.mult, op1=ALU.add), trd + [acc, cpk], [acc])
                        accs.append(acc)
                    gel = fp()
                    op("act", lambda: nc.scalar.activation(out=gel[:, :N], in_=accs[0][:, :N], func=AF.Gelu_apprx_tanh), [accs[0]], [gel])
                    op("dve", lambda: nc.vector.tensor_tensor(out=act_t[j][:, :N], in0=gel[:, :N], in1=accs[1][:, :N], op=ALU.mult), [gel, accs[1]], [act_t[j]])
            if KFFN == 98:
                return
            pds = [K.ps() for _ in range(8)]
            for i in range(3):
                slot = ws_get(f"dn{l}_{i}")
                for c in range(8):
                    for kk in range(8):
                        op("pe", lambda: nc.tensor.matmul(pds[c][:, :N], slot[:, kk * 1024 + c * 128: kk * 1024 + c * 128 + 128], act_t[8 * i + kk][:, :N], start=(i == 0 and kk == 0), stop=(i == 2 and kk == 7)), [slot, act_t[8 * i + kk]], [pds[c]])
            for c in range(8):
                op("dve", lambda: nc.vector.tensor_tensor(out=X[c][:, :N], in0=X[c][:, :N], in1=pds[c][:, :N], op=ALU.add), [X[c], pds[c]], [X[c]])

        def mem_q(pq, l, N):
            qn = K.pool("fa", 4, [128, 516], F32)
            headnorm(pq, cc(f"mqg{l}"), N, qn, False)
            return qn

        def mem_attn_prompt(l, qns, main, N, S):
            interleave([mem_head(l, qns, main, N, S, h) for h in range(4)], 2)

        def mem_head(l, qns, main, N, S, h):
            if True:
                c = h // 2; o = (h % 2) * 64
                qb = bp()
                op("pool", lambda: nc.gpsimd.tensor_copy(out=qb[o:o + 64, :N], in_=qns[c][o:o + 64, :N]), [qns[c]], [qb])
                pts = []
                for mb in range(2):
                    sT = K.ps()
                    op("pe", lambda: nc.tensor.matmul(sT[:, :N], S["kmT"][l][c][o:o + 64, mb * 128:(mb + 1) * 128], qb[o:o + 64, :N], start=True, stop=True), [S["kmT"][l][c], qb], [sT])
                    pT = bp()
                    op("act", lambda: nc.scalar.activation(out=pT[:, :N], in_=sT[:, :N], func=AF.Exp, scale=0.125), [sT], [pT])
                    pts.append(pT)
                yield
                fill(4, N)
                ops_ = K.ps(); dps = K.ps()
                for mb in range(2):
                    op("pe", lambda: nc.tensor.matmul(ops_[:, :N], S["vm"][l][mb][:, c * 128:(c + 1) * 128], pts[mb][:, :N], start=(mb == 0), stop=(mb == 1)), [S["vm"][l][mb], pts[mb]], [ops_])
                for mb in range(2):
                    op("pe", lambda: nc.tensor.matmul(dps[:, :N], ones_bf[:], pts[mb][:, :N], start=(mb == 0), stop=(mb == 1)), [ones_bf, pts[mb]], [dps])
                rd = fp()
                op("act", lambda: nc.scalar.activation(out=rd[o:o + 64, :N], in_=dps[o:o + 64, :N], func=AF.Ln), [dps], [rd])
                op("act", lambda: nc.scalar.activation(out=rd[o:o + 64, :N], in_=rd[o:o + 64, :N], func=AF.Exp, scale=-1.0), [rd], [rd])
                op("dve", lambda: nc.vector.tensor_tensor(out=main[6 + c][o:o + 64, :N], in0=ops_[o:o + 64, :N], in1=rd[o:o + 64, :N], op=ALU.mult), [ops_, rd], [main[6 + c]])

        def sample_attn(Ksrc, Vsrc, Lp, qfm, nq, group, S, main, main0, extra=None):
            F = nq * 64
            nch = F // 128
            W = Lp * 256
            A = S["BUFA"]; T = S["BUFT"] if group > 1 else S["BUFA"]
            qrep = S["QREP"]
            K.dma("sp", A[:, :W], Ksrc, writes=[A])
            for c in range(nch):
                qx = fp()
                op("dve", lambda: nc.vector.tensor_copy(out=qx[:, 0:128].rearrange("p (s m) -> p s m", m=8), in_=qfm[c][:, 0:16].unsqueeze(2).to_broadcast([128, 16, 8])), [qfm[c]], [qx])
                pt = K.ps()
                transpose_to(pt, 0, qx[:, 0:128], qx, 128)
                op("act", lambda: nc.scalar.copy(out=qrep[:, c * 128:(c + 1) * 128], in_=pt[:, 0:128]), [pt], [qrep])
            SC = S["SC"]; PP = S["PP"]; DENP = S["DENP"]; OP = S["OPART"]
            qv = qrep[:, :F].rearrange("p (k g d) -> p k g d", k=4, g=group)
            for g in range(group):
                op("dve", lambda: nc.vector.tensor_tensor(out=T[:, :W].rearrange("p (l k d) -> p l k d", k=4, d=64), in0=A[:, :W].rearrange("p (l k d) -> p l k d", k=4, d=64), in1=qv[:, :, g, :].unsqueeze(1).to_broadcast([128, Lp, 4, 64]), op=ALU.mult), [A, qrep], [T])
                op("dve", lambda: nc.vector.tensor_reduce(out=SC[:, g * Lp * 4:(g + 1) * Lp * 4], in_=T[:, :W].rearrange("p (a d) -> p a d", d=64), axis=AX.X, op=ALU.add), [T], [SC])
            GL = group * Lp * 4
            op("act", lambda: nc.scalar.activation(out=PP[:, :GL], in_=SC[:, :GL], func=AF.Exp, scale=0.125), [SC], [PP])
            for g in range(group):
                op("dve", lambda: nc.vector.tensor_reduce(out=DENP[:, :nq].rearrange("p (k g) -> p k g", g=group)[:, :, g], in_=PP[:, g * Lp * 4:(g + 1) * Lp * 4].rearrange("p (l k) -> p k l", k=4), axis=AX.X, op=ALU.add), [PP], [DENP])
            K.dma("sp", A[:, :W], Vsrc, writes=[A])
            ov = OP[:, :F].rearrange("p (k g d) -> p k g d", k=4, g=group)
            for g in range(group):
                op("dve", lambda: nc.vector.tensor_tensor(out=T[:, :W].rearrange("p (l k d) -> p l k d", k=4, d=64), in0=A[:, :W].rearrange("p (l k d) -> p l k d", k=4, d=64), in1=PP[:, g * Lp * 4:(g + 1) * Lp * 4].rearrange("p (l k) -> p l k", k=4).unsqueeze(3).to_broadcast([128, Lp, 4, 64]), op=ALU.mult), [A, PP], [T])
                op("dve", lambda: nc.vector.tensor_reduce(out=ov[:, :, g, :], in_=T[:, :W].rearrange("p (l k d) -> p k d l", k=4, d=64), axis=AX.X, op=ALU.add), [T], [OP])
            OT = S["OT"]; DT = S["DT"]
            for f0 in range(0, F, 512):
                fw = min(512, F - f0)
                pt = K.ps()
                op("pe", lambda: nc.tensor.matmul(pt[0:16, :fw], sel[:, :], OP[:, f0:f0 + fw], start=True, stop=True), [sel, OP], [pt])
                op("act", lambda: nc.scalar.copy(out=OT[0:16, f0:f0 + fw], in_=pt[0:16, :fw]), [pt], [OT])
            pt = K.ps()
            op("pe", lambda: nc.tensor.matmul(pt[0:16, :nq], sel[:, :], DENP[:, :nq], start=True, stop=True), [sel, DENP], [pt])
            op("act", lambda: nc.scalar.copy(out=DT[0:16, :nq], in_=pt[0:16, :nq]), [pt], [DT])
            if extra is not None:
                extra(OT, DT)
            op("dve", lambda: nc.vector.reciprocal(out=DT[0:16, :nq], in_=DT[0:16, :nq]), [DT], [DT])
            op("dve", lambda: nc.vector.tensor_tensor(out=OT[0:16, :F].rearrange("p (n d) -> p n d", d=64), in0=OT[0:16, :F].rearrange("p (n d) -> p n d", d=64), in1=DT[0:16, :nq].unsqueeze(2).to_broadcast([16, nq, 64]), op=ALU.mult), [OT, DT], [OT])
            for c in range(nch):
                pt = K.ps()
                transpose_to(pt, 0, OT[0:16, c * 128:(c + 1) * 128], OT, 16)
                op("act", lambda: nc.scalar.copy(out=main[main0 + c][:, 0:16], in_=pt[:, 0:16]), [pt], [main[main0 + c]])

        def mixer_a(l, X, hn, main, N, sample, S):
            gs = ws_get(f"g{l}"); s0 = ws_get(f"ina{l}_0", True); s1 = ws_get(f"ina{l}_1", True)

            def inproj(j):
                sl = s0 if j < 7 else s1
                return proj(sl, 896, (j % 7) * 128, hn, N)

            def stage1(c):
                pg = inproj(c)
                gact = K.pool("fa", 4, [128, 516], F32)
                op("act", lambda: nc.scalar.activation(out=gact[:, :N], in_=pg[:, :N], func=AF.Gelu_apprx_tanh), [pg], [gact])
                px = inproj(6 + c)
                acc = K.pool("fa", 4, [128, 516], F32)
                if sample:
                    op("act", lambda: nc.scalar.activation(out=acc[:, :N], in_=px[:, :N], func=AF.Identity, scale=cc(f"rcw{l}_3", c), bias=cc(f"rcb{l}", c)), [px, cpk], [acc])
                else:
                    op("dve", lambda: nc.vector.tensor_scalar(out=acc[:, :N], in0=px[:, :N], scalar1=cc(f"rcw{l}_3", c), scalar2=cc(f"rcb{l}", c), op0=ALU.mult, op1=ALU.add), [px, cpk], [acc])
                if not sample:
                    RH = S["RH"][l]
                    hv = RH[:].rearrange("p (j c) -> p j c", c=6)[:, :, c]
                    xb = fp()
                    op("pool", lambda: nc.gpsimd.tensor_copy(out=xb[:, 0:3], in_=hv), [RH], [xb])
                    op("dve", lambda: nc.vector.tensor_copy(out=xb[:, 3:3 + N], in_=px[:, :N]), [px], [xb])
                    op("pool", lambda: nc.gpsimd.tensor_copy(out=hv, in_=xb[:, N:N + 3]), [xb], [RH])
                    taps = [(xb[:, 2:2 + N], [xb]), (xb[:, 1:1 + N], [xb]), (xb[:, 0:N], [xb])]
                else:
                    rst = S["RST"][l]
                    rv = rst[:].rearrange("p (j c s) -> p j c s", j=3, c=6)
                    op("act", lambda: nc.scalar.copy(out=S["XRN"][l][:].rearrange("p (c s) -> p c s", c=6)[:, c, :], in_=px[:, :N]), [px], [S["XRN"][l]])
                    taps = [(rv[:, 2, c, :], [rst]), (rv[:, 1, c, :], [rst]), (rv[:, 0, c, :], [rst])]
                for ti, (tap, trd) in enumerate(taps):
                    wname = f"rcw{l}_{2 - ti}"
                    op("dve", lambda: nc.vector.scalar_tensor_tensor(out=acc[:, :N], in0=tap, scalar=cc(wname, c), in1=acc[:, :N], op0=ALU.mult, op1=ALU.add), trd + [acc, cpk], [acc])
                xcb = bp()
                op("pool", lambda: nc.gpsimd.tensor_copy(out=xcb[:, :N], in_=acc[:, :N]), [acc], [xcb])
                fill(10, N)
                pgx = K.ps(); pga = K.ps()
                op("pe", lambda: nc.tensor.matmul(pgx[:, :N], gs[:, c * 128:(c + 1) * 128], xcb[:, :N], start=True, stop=True), [gs, xcb], [pgx])
                op("pe", lambda: nc.tensor.matmul(pga[:, :N], gs[:, 768 + c * 128:768 + (c + 1) * 128], xcb[:, :N], start=True, stop=True), [gs, xcb], [pga])
                return gact, acc, pgx, pga

            def stage2(c, gact, acc, pgx, pga):
                k = l * 6 + c
                gx = fp(); ga = fp(); aa = fp(); a2 = fp()
                op("act", lambda: nc.scalar.activation(out=gx[:, :N], in_=pgx[:, :N], func=AF.Tanh, scale=0.5, bias=hbx[:, k:k + 1]), [pgx, hbx], [gx])
                op("act", lambda: nc.scalar.activation(out=ga[:, :N], in_=pga[:, :N], func=AF.Tanh, scale=0.5, bias=hba[:, k:k + 1]), [pga, hba], [ga])
                op("act", lambda: nc.scalar.activation(out=aa[:, :N], in_=ga[:, :N], func=AF.Exp, scale=hclt[:, k:k + 1], bias=hclt[:, k:k + 1]), [ga, hclt], [aa])
                op("act", lambda: nc.scalar.activation(out=a2[:, :N], in_=ga[:, :N], func=AF.Exp, scale=clt[:, k:k + 1], bias=clt[:, k:k + 1]), [ga, clt], [a2])
                op("act", lambda: nc.scalar.activation(out=a2[:, :N], in_=a2[:, :N], func=AF.Ln, scale=-1.0, bias=1.0), [a2], [a2])
                op("act", lambda: nc.scalar.activation(out=a2[:, :N], in_=a2[:, :N], func=AF.Exp, scale=0.5, bias=lnhalf[:, 0:1]), [a2, lnhalf], [a2])
                op("dve", lambda: nc.vector.scalar_tensor_tensor(out=gx[:, :N], in0=gx[:, :N], scalar=1.0, in1=acc[:, :N], op0=ALU.add, op1=ALU.mult), [gx, acc], [gx])
                op("dve", lambda: nc.vector.tensor_tensor(out=gx[:, :N], in0=gx[:, :N], in1=a2[:, :N], op=ALU.mult), [gx, a2], [gx])
                hh = ga
                if not sample:
                    HS = S["HS"][l]
                    op("dve", lambda: nc.vector.tensor_tensor_scan(out=hh[:, :N], data0=aa[:, :N], data1=gx[:, :N], initial=HS[:, c:c + 1], op0=ALU.mult, op1=ALU.add), [aa, gx, HS], [hh])
                    op("pool", lambda: nc.gpsimd.tensor_copy(out=HS[:, c:c + 1], in_=hh[:, N - 1:N]), [hh], [HS])
                else:
                    h0 = S["H0"][l][:].rearrange("p (c s) -> p c s", c=6)[:, c, :]
                    op("dve", lambda: nc.vector.tensor_tensor(out=hh[:, :N], in0=aa[:, :N], in1=h0, op=ALU.mult), [aa, S["H0"][l]], [hh])
                    op("dve", lambda: nc.vector.tensor_tensor(out=hh[:, :N], in0=hh[:, :N], in1=gx[:, :N], op=ALU.add), [hh, gx], [hh])
                    op("pool", lambda: nc.gpsimd.tensor_copy(out=S["HNEW"][l][:].rearrange("p (c s) -> p c s", c=6)[:, c, :], in_=hh[:, :N]), [hh], [S["HNEW"][l]])
                op("dve", lambda: nc.vector.tensor_tensor(out=main[c][:, :N], in0=hh[:, :N], in1=gact[:, :N], op=ALU.mult), [hh, gact], [main[c]])

            prev = stage1(0)
            for c in range(6):
                nxt = stage1(c + 1) if c + 1 < 6 else None
                stage2(c, *prev)
                prev = nxt
            qns = []
            for c in range(2):
                pq = inproj(12 + c)
                qns.append(mem_q(pq, l, N))
            return qns

        def out_proj(l, X, main, N):
            slot = ws_get(f"out{l}")
            for c in range(8):
                pt = proj(slot, 1024, c * 128, main, N)
                op("dve", lambda: nc.vector.tensor_tensor(out=X[c][:, :N], in0=X[c][:, :N], in1=pt[:, :N], op=ALU.add), [X[c], pt], [X[c]])

        def shared_kv(X, hn, N, sample, S, ci):
            rmsnorm(X, "kvg", hn, N)
            slot = ws_get("kv")
            if not sample:
                kT = S["kT"]; Vt = S["Vtok"]
                if ci > 0:
                    for kv in range(4):
                        op("pool", lambda: nc.gpsimd.tensor_copy(out=kT[kv][:, 0:128], in_=kT[kv][:, 512:640]), [kT[kv]], [kT[kv]])
                    op("pool", lambda: nc.gpsimd.tensor_copy(out=Vt[0][:, :], in_=Vt[4][:, :]), [Vt[4]], [Vt[0]])
            KKV = int(os.environ.get("KKV", "9")) if not sample else 9
            for kv in range(4):
                if KKV < 2:
                    break
                pk = proj(slot, 768, kv * 128, hn, N)
                kn = fp()
                headnorm(pk, cc("kng"), N, kn, False)
                if KKV < 3:
                    continue
                if not sample:
                    t1, t2 = rope(kn, S["cos"][:, :N], S["sin"][:, :N], [S["cos"], S["sin"]], S["kT"][kv][:, 128:128 + N], S["kT"][kv], N, False)
                    if ci == NCH - 1:
                        op("pool", lambda: nc.gpsimd.tensor_tensor(out=S["KF"][:, kv * 128:(kv + 1) * 128], in0=t1[:, N - 128:N], in1=t2[:, N - 128:N], op=ALU.add), [t1, t2], [S["KF"]])
                else:
                    rope(kn, rcs_t[:, 0:1], rcs_t[:, 1:2], [rcs_t], S["KNEW"][:, kv * 16:(kv + 1) * 16], S["KNEW"], N, True)
            if not sample:
                pass
            if KKV < 4:
                return
            nb = max(1, N // 128)
            for tb in range(nb):
                w = min(128, N)
                pv = K.ps()
                for k in range(8):
                    op("pe", lambda: nc.tensor.matmul(pv[0:w, 0:256], hn[k][:, tb * 128:tb * 128 + w], slot[:, k * 768 + 512:k * 768 + 768], start=(k == 0), stop=(k == 7)), [hn[k], slot], [pv])
                if not sample:
                    op("act", lambda: nc.scalar.copy(out=S["Vtok"][tb + 1][:, 64:320], in_=pv[:, 0:256]), [pv], [S["Vtok"][tb + 1]])
                    if ci == NCH - 1 and tb == nb - 1:
                        op("act", lambda: nc.scalar.copy(out=S["VF"][:, :], in_=pv[:, 0:256]), [pv], [S["VF"]])
                else:
                    op("act", lambda: nc.scalar.copy(out=S["VNEW"][0:16, :], in_=pv[0:16, 0:256]), [pv], [S["VNEW"]])

        def mixer_b_q(l, hn, N, sample, S):
            slot = ws_get(f"inb{l}")
            qts = [None] * 6
            qns = [None] * 2

            def qchain(c):
                pq = proj(slot, 1024, c * 128, hn, N)
                qn = fp()
                yield from headnorm_g(pq, cc(f"qg{l}"), N, qn)
                if not sample:
                    qt = S["qT"][c]
                    rope(qn, S["cos"][:, :N], S["sin"][:, :N], [S["cos"], S["sin"]], qt[:, :N], qt, N, False)
                else:
                    qt = S["QS"][c]
                    rope(qn, rcs_t[:, 0:1], rcs_t[:, 1:2], [rcs_t], qt[:, :N], qt, N, True)
                qts[c] = qt

            def mchain(c):
                pq = proj(slot, 1024, 768 + c * 128, hn, N)
                qn = K.pool("fa", 4, [128, 516], F32)
                yield from headnorm_g(pq, cc(f"mqg{l}"), N, qn)
                qns[c] = qn

            interleave([qchain(c) for c in range(6)] + [mchain(c) for c in range(2)], 2)
            return qts, qns

        def swa_prompt(l, qT, main, S, ci):
            kT = S["kT"]; Vt = S["Vtok"]
            interleave([swa_head(l, qT, main, S, ci, n) for n in range(12)], 2)

        def swa_head(l, qT, main, S, ci, n):
            kT = S["kT"]; Vt = S["Vtok"]
            if True:
                kv = n // 3; o = (n % 2) * 64; qc = n // 2
                PM = []
                for jb in range(5):
                    j = jb - 1
                    if j == -1:
                        if ci == 0:
                            PM.append(None)
                            continue
                        q0, nn, m0 = 0, 128, 128
                    elif j == 3:
                        q0, nn, m0 = 384, 128, 0
                    else:
                        q0, nn, m0 = j * 128, 256, 0
                    sT = K.ps()
                    op("pe", lambda: nc.tensor.matmul(sT[:, :nn], kT[kv][o:o + 64, jb * 128:(jb + 1) * 128], qT[qc][o:o + 64, q0:q0 + nn], start=True, stop=True), [kT[kv], qT[qc]], [sT])
                    pT = bp()
                    op("act", lambda: nc.scalar.activation(out=pT[:, :nn], in_=sT[:, :nn], func=AF.Exp, scale=0.125), [sT], [pT])
                    pm = K.pool("pm", 12, [128, 256], BF16)
                    if (jb + n) % 2 == 0:
                        op("pool", lambda: nc.gpsimd.tensor_tensor(out=pm[:, :nn], in0=pT[:, :nn], in1=mask_bf[:, m0:m0 + nn], op=ALU.mult), [pT, mask_bf], [pm])
                    else:
                        op("dve", lambda: nc.vector.tensor_tensor(out=pm[:, :nn], in0=pT[:, :nn], in1=mask_bf[:, m0:m0 + nn], op=ALU.mult), [pT, mask_bf], [pm])
                    PM.append(pm)
                yield
                fill(5)
                ops_ = K.ps(); dps = K.ps()
                vs = 64 + kv * 64 - o
                for which, lhs_of in ((ops_, lambda blk: Vt[blk][:, vs:vs + 128]), (dps, lambda blk: ones_bf[:])):
                    for i in range(4):
                        parts = []
                        if PM[i] is not None:
                            parts.append((i, PM[i][:, 0:128] if i == 0 else PM[i][:, 128:256], PM[i]))
                        parts.append((i + 1, PM[i + 1][:, 0:128], PM[i + 1]))
                        for pi, (blk, rhs, pmt) in enumerate(parts):
                            rd_t = [pmt] + ([Vt[blk]] if which is ops_ else [ones_bf])
                            op("pe", lambda: nc.tensor.matmul(which[:, i * 128:(i + 1) * 128], lhs_of(blk), rhs, start=(pi == 0), stop=(pi == len(parts) - 1)), rd_t, [which])
                yield
                rd = fp()
                op("act", lambda: nc.scalar.activation(out=rd[o:o + 64, :TC], in_=dps[o:o + 64, :TC], func=AF.Ln, bias=esink[o:o + 64, (l - 2) * 12 + n:(l - 2) * 12 + n + 1]), [dps, esink], [rd])
                op("act", lambda: nc.scalar.activation(out=rd[o:o + 64, :TC], in_=rd[o:o + 64, :TC], func=AF.Exp, scale=-1.0), [rd], [rd])
                op("dve", lambda: nc.vector.tensor_tensor(out=main[qc][o:o + 64, :TC], in0=ops_[o:o + 64, :TC], in1=rd[o:o + 64, :TC], op=ALU.mult), [ops_, rd], [main[qc]])

        def mem_prologue(S, es2):
            MIN = K.sb("memin", [128, 2 * D], F32, es2)
            rowm = K.sb("rowm", [128, 1024], F32, es2)
            K.dma("sp", rowm[:], rowc[0:1024].partition_broadcast(128), writes=[rowm])
            for mb in range(2):
                K.dma("sp", MIN[:, mb * D:(mb + 1) * D], memp[mb * 128:(mb + 1) * 128, :], writes=[MIN])
            MT = [K.sb(f"memT{c}", [128, 256], F32, es2) for c in range(8)]
            for c in range(8):
                pt = K.ps()
                for mb in range(2):
                    transpose_to(pt, mb * 128, MIN[:, mb * D + c * 128: mb * D + (c + 1) * 128], MIN, 128)
                op("act", lambda: nc.scalar.copy(out=MT[c][:, :], in_=pt[:, 0:256]), [pt], [MT[c]])
            ss = K.ps()
            for c in range(8):
                sq = bp()
                op("act", lambda: nc.scalar.activation(out=sq[:, :256], in_=MT[c][:, :], func=AF.Square), [MT[c]], [sq])
                op("pe", lambda: nc.tensor.matmul(ss[:, :256], ones_bf[:], sq[:, :256], start=(c == 0), stop=(c == 7)), [sq, ones_bf], [ss])
            rb = K.sb("memrb", [128, 256], F32, es2)
            op("act", lambda: nc.scalar.activation(out=rb[:, :], in_=ss[:, :256], func=AF.Ln, scale=1.0 / D, bias=EPS), [ss], [rb])
            op("act", lambda: nc.scalar.activation(out=rb[:, :], in_=rb[:, :], func=AF.Exp, scale=-0.5), [rb], [rb])
            MN = [K.sb(f"memn{c}", [128, 256], BF16, es2) for c in range(8)]
            WM = K.sb("wmem_f", [128, 4096], F32, es2)
            WMB = K.sb("wmem_b", [128, 4096], BF16, es2)
            KTOK = K.sb("ktok", [128, 256], F32, es2)
            VTOK = K.sb("vtok", [128, 256], F32, es2)
            KSQ = K.sb("ksq", [128, 256], F32, es2)
            KR = K.sb("kr", [128, 4], F32, es2)
            for l in range(4):
                K.dma("sp", WM[:, :], wmem[l], writes=[WM])
                op("pool", lambda: nc.gpsimd.tensor_copy(out=WMB[:, :], in_=WM[:, :]), [WM], [WMB])
                for c in range(8):
                    op("dve", lambda: nc.vector.scalar_tensor_tensor(out=MN[c][:, :], in0=MT[c][:, :], scalar=cc(f"memg{l}", c), in1=rb[:, :], op0=ALU.mult, op1=ALU.mult), [MT[c], rb, cpk], [MN[c]])
                for mb in range(2):
                    pt = K.ps()
                    for k in range(8):
                        op("pe", lambda: nc.tensor.matmul(pt[:, :512], MN[k][:, mb * 128:(mb + 1) * 128], WMB[:, k * 512:(k + 1) * 512], start=(k == 0), stop=(k == 7)), [MN[k], WMB], [pt])
                    op("act", lambda: nc.scalar.copy(out=VTOK[:, :], in_=pt[:, 256:512]), [pt], [VTOK])
                    op("pool", lambda: nc.gpsimd.tensor_copy(out=S["vm"][l][mb][:, :], in_=VTOK[:, :]), [VTOK], [S["vm"][l][mb]])
                    K.dma("pool", o_pmv[l, mb * 128:(mb + 1) * 128, :], VTOK[:, :], reads=[VTOK])
                    op("act", lambda: nc.scalar.activation(out=KSQ[:, :], in_=pt[:, 0:256], func=AF.Square), [pt], [KSQ])
                    op("dve", lambda: nc.vector.tensor_reduce(out=KR[:, :], in_=KSQ[:, :].rearrange("p (h d) -> p h d", d=64), axis=AX.X, op=ALU.add), [KSQ], [KR])
                    op("act", lambda: nc.scalar.activation(out=KR[:, :], in_=KR[:, :], func=AF.Sqrt, scale=1.0 / HD, bias=EPS), [KR], [KR])
                    op("dve", lambda: nc.vector.reciprocal(out=KR[:, :], in_=KR[:, :]), [KR], [KR])
                    op("dve", lambda: nc.vector.tensor_tensor(out=KTOK[:, :].rearrange("p (h d) -> p h d", d=64), in0=pt[:, 0:256].rearrange("p (h d) -> p h d", d=64), in1=KR[:, :].unsqueeze(2).to_broadcast([128, 4, 64]), op=ALU.mult), [pt, KR], [KTOK])
                    op("dve", lambda: nc.vector.tensor_tensor(out=KTOK[:, :], in0=KTOK[:, :], in1=rowm[:, l * 256:(l + 1) * 256], op=ALU.mult), [KTOK, rowm], [KTOK])
                    K.dma("pool", o_pmk[l, mb * 128:(mb + 1) * 128, :], KTOK[:, :], reads=[KTOK])
                    for c in range(2):
                        p2 = K.ps()
                        transpose_to(p2, 0, KTOK[:, c * 128:(c + 1) * 128], KTOK, 128)
                        op("act", lambda: nc.scalar.copy(out=S["kmT"][l][c][:, mb * 128:(mb + 1) * 128], in_=p2[:, 0:128]), [p2], [S["kmT"][l][c]])

        def sample_pass():
            add_pass(False)
            with contextlib.ExitStack() as es2:
                S = {}
                N = NS
                X = [K.sb(f"sx{c}", [128, NS], F32, es2) for c in range(8)]
                hn = [K.sb(f"shn{c}", [128, NS], BF16, es2) for c in range(8)]
                main = [K.sb(f"smain{c}", [128, NS], BF16, es2) for c in range(8)]
                act_t = [K.sb(f"sact{c}", [128, NS], BF16, es2) for c in range(24)]
                S["BUFA"] = K.sb("bufa", [128, 8192], F32, es2)
                S["BUFT"] = K.sb("buft", [128, 4096], F32, es2)
                S["QREP"] = K.sb("qrep", [128, 768], F32, es2)
                S["SC"] = K.sb("sc", [128, 192], F32, es2); S["PP"] = K.sb("pp", [128, 192], F32, es2)
                S["DENP"] = K.sb("denp", [128, 12], F32, es2); S["OPART"] = K.sb("opart", [128, 768], F32, es2)
                S["OT"] = K.sb("ot", [16, 768], F32, es2); S["DT"] = K.sb("dt", [16, 12], F32, es2)
                S["RST"] = [K.sb(f"rst{l}", [128, 18 * 16], F32, es2) for l in range(2)]
                S["H0"] = [K.sb(f"h0{l}", [128, 6 * 16], F32, es2) for l in range(2)]
                S["HNEW"] = [K.sb(f"hnew{l}", [128, 6 * 16], F32, es2) for l in range(2)]
                S["XRN"] = [K.sb(f"xrn{l}", [128, 6 * 16], F32, es2) for l in range(2)]
                S["FST"] = K.sb("fst", [128, 96 * 16], F32, es2)
                S["UNEW"] = K.sb("unew", [128, 48 * 16], F32, es2)
                S["KNEW"] = K.sb("knew", [128, 4 * 16], F32, es2)
                S["VNEW"] = K.sb("vnew", [16, 256], F32, es2)
                S["QS"] = [K.sb(f"qs{c}", [128, NS], F32, es2) for c in range(6)]
                STG = K.sb("stg", [16, DFF], F32, es2)
                TOK = STG
                KTOKS = K.sb("ktoks", [16, 256], F32, es2)
                QTOK = K.sb("qtok", [16, 768], F32, es2)
                ENEW = K.sb("enew", [16, 12], F32, es2)
                ETMP = S["BUFT"]

                def load_fm(dst, src_ap, ncols_total):
                    nchk = ncols_total // 128
                    for c0 in range(0, nchk, 32):
                        cn = min(32, nchk - c0)
                        pt = K.ps()
                        for c in range(cn):
                            transpose_to(pt, c * 16, src_ap(c0 + c), src_ap.tile, 16)
                        op("act", lambda: nc.scalar.copy(out=dst[:, c0 * 16:(c0 + cn) * 16], in_=pt[:, 0:cn * 16]), [pt], [dst])

                class Src:
                    def __init__(self, tile, base=0):
                        self.tile = tile
                        self.base = base

                    def __call__(self, c):
                        return self.tile[0:16, self.base + c * 128:self.base + (c + 1) * 128]

                def store_tm(dst_ap_fn, src_tile, nchk, stage, cbase=0):
                    for c0 in range(0, nchk, 4):
                        cn = min(4, nchk - c0)
                        pt = K.ps()
                        for c in range(cn):
                            transpose_to(pt, c * 128, src_tile[:, (cbase + c0 + c) * 16:(cbase + c0 + c + 1) * 16], src_tile, 128)
                        op("act", lambda: nc.scalar.copy(out=stage[0:16, c0 * 128:(c0 + cn) * 128], in_=pt[0:16, 0:cn * 128]), [pt], [stage])
                    K.dma("pool", dst_ap_fn, stage[0:16, 0:nchk * 128], reads=[stage])

                K.dma("sp", STG[0:16, 0:D], xs[:, :], writes=[STG])
                XS = K.sb("xs_fm", [128, 8 * 16], F32, es2)
                load_fm(XS, Src(STG), D)
                for c in range(8):
                    op("pool", lambda: nc.gpsimd.tensor_copy(out=X[c][:, :], in_=XS[:, c * 16:(c + 1) * 16]), [XS], [X[c]])
                for l in range(2):
                    K.dma("sp", TOK[0:16, 0:3 * DRNN], st_rc[l], writes=[TOK])
                    load_fm(S["RST"][l], Src(TOK), 3 * DRNN)
                    K.dma("sp", TOK[0:16, 0:DRNN], st_h[l], writes=[TOK])
                    load_fm(S["H0"][l], Src(TOK), DRNN)
                    pz = K.pseudo(f"src_copy{l}")
                    K.dma("pool", o_src[l, :, 0:2 * DRNN], st_rc[l, :, DRNN:3 * DRNN], semtile=pz)
                for l in range(4):
                    pz = K.pseudo(f"sfc_copy{l}")
                    K.dma("pool", o_sfc[l, :, 0:2 * DFF], st_fc[l, :, 2 * DFF:4 * DFF], semtile=pz)

                for l in range(4):
                    rmsnorm(X, f"gmix{l}", hn, N)
                    if l < 2:
                        qns = mixer_a(l, X, hn, main, N, True, S)
                    else:
                        qts, qns = mixer_b_q(l, hn, N, True, S)
                    sample_attn(c_mk[l], c_mv[l], 32, qns, 4, 1, S, main, 6)
                    if l >= 2:
                        pt = K.ps(); pt2 = K.ps()
                        for c in range(6):
                            transpose_to(pt if c < 4 else pt2, (c % 4) * 128, qts[c][:, 0:16], qts[c], 128)
                        op("act", lambda: nc.scalar.copy(out=QTOK[0:16, 0:512], in_=pt[0:16, 0:512]), [pt], [QTOK])
                        op("act", lambda: nc.scalar.copy(out=QTOK[0:16, 512:768], in_=pt2[0:16, 0:256]), [pt2], [QTOK])

                        def extra(OT, DT, l=l):
                            qv = QTOK[0:16, :].rearrange("p (k g d) -> p k g d", k=4, g=3)
                            kb_ = KTOKS[0:16, :].rearrange("p (k d) -> p k d", k=4).unsqueeze(2).to_broadcast([16, 4, 3, 64])
                            vb_ = S["VNEW"][0:16, :].rearrange("p (k d) -> p k d", k=4).unsqueeze(2).to_broadcast([16, 4, 3, 64])
                            ev = ETMP[0:16, 0:768].rearrange("p (k g d) -> p k g d", k=4, g=3)
                            op("dve", lambda: nc.vector.tensor_tensor(out=ev, in0=qv, in1=kb_, op=ALU.mult), [QTOK, KTOKS], [ETMP])
                            op("dve", lambda: nc.vector.tensor_reduce(out=ENEW[0:16, :], in_=ETMP[0:16, 0:768].rearrange("p (n d) -> p n d", d=64), axis=AX.X, op=ALU.add), [ETMP], [ENEW])
                            op("act", lambda: nc.scalar.activation(out=ENEW[0:16, :], in_=ENEW[0:16, :], func=AF.Exp, scale=0.125), [ENEW], [ENEW])
                            op("dve", lambda: nc.vector.tensor_tensor(out=ev, in0=ENEW[0:16, :].rearrange("p (k g) -> p k g", g=3).unsqueeze(3).to_broadcast([16, 4, 3, 64]), in1=vb_, op=ALU.mult), [ENEW, S["VNEW"]], [ETMP])
                            op("dve", lambda: nc.vector.tensor_tensor(out=OT[0:16, :], in0=OT[0:16, :], in1=ETMP[0:16, 0:768], op=ALU.add), [OT, ETMP], [OT])
                            op("dve", lambda: nc.vector.tensor_tensor(out=DT[0:16, :], in0=DT[0:16, :], in1=ENEW[0:16, :], op=ALU.add), [DT, ENEW], [DT])
                            op("dve", lambda: nc.vector.tensor_tensor(out=DT[0:16, :], in0=DT[0:16, :], in1=esink[0:16, (l - 2) * 12:(l - 2) * 12 + 12], op=ALU.add), [DT, esink], [DT])

                        sample_attn(c_sk[:, :], c_sv[:, :], 16, qts, 12, 3, S, main, 0, extra)
                    out_proj(l, X, main, N)
                    rmsnorm(X, f"gffn{l}", hn, N)
                    for jq in range(4):
                        K.dma("sp", STG[0:16, :], st_fc[l, :, jq * DFF:(jq + 1) * DFF], writes=[STG])
                        pt = K.ps()
                        for c in range(24):
                            transpose_to(pt, c * 16, STG[0:16, c * 128:(c + 1) * 128], STG, 16)
                        op("act", lambda: nc.scalar.copy(out=S["FST"][:, jq * 24 * 16:(jq + 1) * 24 * 16], in_=pt[:, 0:24 * 16]), [pt], [S["FST"]])
                    ffn(l, X, hn, act_t, N, True, S)
                    for hf in range(2):
                        store_tm(o_sfc[l, :, 2 * DFF + hf * DFF:2 * DFF + (hf + 1) * DFF], S["UNEW"], 24, STG, hf * 24)
                    if l < 2:
                        store_tm(o_sh[l], S["HNEW"][l], 6, TOK)
                        store_tm(o_src[l, :, 2 * DRNN:3 * DRNN], S["XRN"][l], 6, TOK)
                    if l == 1:
                        shared_kv(X, hn, N, True, S, 0)
                        pt = K.ps()
                        for kv in range(4):
                            transpose_to(pt, kv * 128, S["KNEW"][:, kv * 16:(kv + 1) * 16], S["KNEW"], 128)
                        op("act", lambda: nc.scalar.copy(out=KTOKS[0:16, :].rearrange("p (k d) -> p k d", d=64), in_=pt[0:16, 0:512].rearrange("p (k e) -> p k e", e=128)[:, :, 0:64]), [pt], [KTOKS])
                        K.dma("pool", o_sk[:, :], KTOKS[0:16, :], reads=[KTOKS])
                        K.dma("pool", o_sv[:, :], S["VNEW"][0:16, :], reads=[S["VNEW"]])
                XO = K.sb("xs_out", [128, 8 * 16], F32, es2)
                for c in range(8):
                    op("pool", lambda: nc.gpsimd.tensor_copy(out=XO[:, c * 16:(c + 1) * 16], in_=X[c][:, :]), [X[c]], [XO])
                store_tm(y_s[:, :], XO, 8, STG)
                K.barrier()

        def prompt_pass():
            with contextlib.ExitStack() as es2:
                S = {}
                S["kmT"] = [[K.sb(f"kmT{l}_{c}", [128, 256], BF16, es2) for c in range(2)] for l in range(4)]
                S["vm"] = [[K.sb(f"vm{l}_{m}", [128, 256], BF16, es2) for m in range(2)] for l in range(4)]
                with contextlib.ExitStack() as es3:
                    mem_prologue(S, es3)
                    K.barrier()
                N = TC
                X = [K.sb(f"x{c}", [128, TC], F32, es2) for c in range(8)]
                hn = [K.sb(f"hn{c}", [128, TC], BF16, es2) for c in range(8)]
                main = [K.sb(f"main{c}", [128, TC], BF16, es2) for c in range(8)]
                act_t = [K.sb(f"act{c}", [128, TC], BF16, es2) for c in range(24)]
                S["HIST"] = [K.sb(f"hist{l}", [128, 96], F32, es2) for l in range(4)]
                S["RH"] = [K.sb(f"rh{l}", [128, 18], F32, es2) for l in range(2)]
                S["HS"] = [K.sb(f"hs{l}", [128, 6], F32, es2) for l in range(2)]
                S["kT"] = [K.sb(f"kT{kv}", [128, 640], BF16, es2) for kv in range(4)]
                S["Vtok"] = [K.sb(f"vt{i}", [128, 384], BF16, es2) for i in range(5)]
                S["qT"] = act_t[0:6]
                S["cos"] = K.sb("cos", [128, TC], F32, es2); S["sin"] = K.sb("sin", [128, TC], F32, es2)
                S["VF"] = K.sb("vf", [128, 256], F32, es2)
                S["KF"] = K.sb("kf", [128, 4 * 128], F32, es2)
                XIN = K.sb("xin", [128, 2 * D], F32, es2)
                for l in range(4):
                    op("pool", lambda: nc.gpsimd.memset(S["HIST"][l][:, :], 0.0), [], [S["HIST"][l]])
                for l in range(2):
                    op("pool", lambda: nc.gpsimd.memset(S["RH"][l][:, :], 0.0), [], [S["RH"][l]])
                    op("pool", lambda: nc.gpsimd.memset(S["HS"][l][:, :], 0.0), [], [S["HS"][l]])
                for i in range(5):
                    op("pool", lambda: nc.gpsimd.memset(S["Vtok"][i][:, :], 0.0), [], [S["Vtok"][i]])

                def load_x(ci, hf):
                    for tb in range(2):
                        K.dma("sp", XIN[:, tb * D:(tb + 1) * D], xp[ci * TC + (hf * 2 + tb) * 128: ci * TC + (hf * 2 + tb + 1) * 128, :], writes=[XIN])

                load_x(0, 0)
                for ci in range(NCH):
                    add_pass(ci == 0)
                    xh = []
                    for tb in range(2):
                        for fh in range(2):
                            t_ = fp()
                            K.dma("sp", t_[:, 0:512], xp[ci * TC + (2 + tb) * 128: ci * TC + (3 + tb) * 128, fh * 512:(fh + 1) * 512], writes=[t_])
                            xh.append(t_)
                    for c in range(8):
                        pt = K.ps()
                        for tb in range(2):
                            transpose_to(pt, tb * 128, XIN[:, tb * D + c * 128: tb * D + (c + 1) * 128], XIN, 128)
                        for tb in range(2):
                            t_ = xh[tb * 2 + c // 4]
                            transpose_to(pt, (2 + tb) * 128, t_[:, (c % 4) * 128:(c % 4 + 1) * 128], t_, 128)
                        op("act", lambda: nc.scalar.copy(out=X[c][:, :], in_=pt[:, :]), [pt], [X[c]])
                    if ci + 1 < NCH:
                        load_x(ci + 1, 0)
                    for l in range(4):
                        rmsnorm(X, f"gmix{l}", hn, N)
                        if l < 2:
                            qns = mixer_a(l, X, hn, main, N, False, S)
                        else:
                            qts, qns = mixer_b_q(l, hn, N, False, S)
                        if l >= 2:
                            interleave2([swa_head(l, qts, main, S, ci, n) for n in range(12)], 2,
                                        [mem_head(l, qns, main, N, S, h) for h in range(4)], 1)
                        else:
                            mem_attn_prompt(l, qns, main, N, S)
                        out_proj(l, X, main, N)
                        rmsnorm(X, f"gffn{l}", hn, N)
                        ffn(l, X, hn, act_t, N, False, S)
                        if l == 1:
                            K.dma("sp", S["cos"][:, :], rcos[:, ci * TC:(ci + 1) * TC], writes=[S["cos"]])
                            K.dma("sp", S["sin"][:, :], rsin[:, ci * TC:(ci + 1) * TC], writes=[S["sin"]])
                            shared_kv(X, hn, N, False, S, ci)
                    for tb in range(4):
                        for hlf in range(2):
                            pt = K.ps()
                            for c4 in range(4):
                                c = hlf * 4 + c4
                                transpose_to(pt, c4 * 128, X[c][:, tb * 128:(tb + 1) * 128], X[c], 128)
                            yo = fp()
                            op("act", lambda: nc.scalar.copy(out=yo[:, 0:512], in_=pt[:, :]), [pt], [yo])
                            K.dma("pool", y_p[ci * TC + tb * 128: ci * TC + (tb + 1) * 128, hlf * 512:(hlf + 1) * 512], yo[:, 0:512], reads=[yo])
                FIN = K.sb("fin", [128, 128], F32, es2)
                for l in range(4):
                    pt = K.ps()
                    transpose_to(pt, 0, S["HIST"][l][:, 0:96], S["HIST"][l], 128)
                    op("act", lambda: nc.scalar.copy(out=FIN[0:96, :], in_=pt[0:96, 0:128]), [pt], [FIN])
                    for j in range(2):
                        K.dma("pool", o_pfc[l, j, :].rearrange("(c f) -> c f", f=128), FIN[j * 48:(j + 1) * 48, :], reads=[FIN])
                for l in range(2):
                    pt = K.ps()
                    transpose_to(pt, 0, S["RH"][l][:, 0:18], S["RH"][l], 128)
                    op("act", lambda: nc.scalar.copy(out=FIN[0:18, :], in_=pt[0:18, 0:128]), [pt], [FIN])
                    for j in range(3):
                        K.dma("pool", o_prc[l, j, :].rearrange("(c f) -> c f", f=128), FIN[j * 6:(j + 1) * 6, :], reads=[FIN])
                    pt = K.ps()
                    transpose_to(pt, 0, S["HS"][l][:, 0:6], S["HS"][l], 128)
                    op("act", lambda: nc.scalar.copy(out=FIN[0:6, :], in_=pt[0:6, 0:128]), [pt], [FIN])
                    K.dma("pool", o_ph[l, :].rearrange("(c f) -> c f", f=128), FIN[0:6, :], reads=[FIN])
                K.dma("pool", o_pv[:, :], S["VF"][:, :], reads=[S["VF"]])
                KO = K.sb("ko", [128, 256], F32, es2)
                pt = K.ps()
                for kv in range(4):
                    transpose_to(pt, kv * 128, S["KF"][:, kv * 128:(kv + 1) * 128], S["KF"], 128)
                op("act", lambda: nc.scalar.copy(out=KO[:, :].rearrange("p (k d) -> p k d", d=64), in_=pt[:, 0:512].rearrange("p (k e) -> p k e", e=128)[:, :, 0:64]), [pt], [KO])
                K.dma("pool", o_pk[:, :], KO[:, :], reads=[KO])
                K.barrier()

        prompt_pass()
        sample_pass()
        K.finish()
    return nc


build.cidx = None


def kernel(**inp):
    inp = {k: np.asarray(v) for k, v in inp.items()}
    B, SEQ, _ = inp["x_prompt"].shape
    DB = inp["x_sample"].shape[0]
    ncores = DB // NS
    wpack, _ = make_wpack(inp)
    cpk, cidx = make_cpk(inp)
    cpk_full = np.zeros((128, 1000), np.float32)
    cpk_full[:, :cpk.shape[1]] = cpk
    build.cidx = cidx
    nc = build(SEQ)
    cm = const_mats()
    rcos, rsin = rope_tables(np.arange(SEQ))
    cs, sn = rope_tables(np.array([PAST]))
    rcs_s = np.ascontiguousarray(np.concatenate([cs, sn], axis=1))
    rowc = np.concatenate([np.tile(inp["mem_k_norm_g"][l], 4) for l in range(4)] + [inp["sinks"].reshape(-1)]).astype(np.float32)
    wmem = np.stack([_kpack(inp["w_mem_kv"][l]) for l in range(4)])
    in_maps = []
    for c in range(ncores):
        b = c * B // ncores
        s0 = c * NS
        sl = slice(s0, s0 + NS)
        m = dict(
            xp=np.ascontiguousarray(inp["x_prompt"][b]), xs=np.ascontiguousarray(inp["x_sample"][sl, 0, :]),
            memp=np.ascontiguousarray(inp["mem_prompt"][b]),
            st_h=np.ascontiguousarray(inp["state_rglru_h"][:, sl]),
            st_rc=np.ascontiguousarray(inp["state_rglru_conv"][:, sl].reshape(2, NS, 3 * DRNN)),
            st_fc=np.ascontiguousarray(inp["state_ffn_conv"][:, sl].reshape(4, NS, 4 * DFF)),
            c_sk=np.ascontiguousarray(inp["cache_swa_k"][sl].reshape(NS * 8, 16 * 256)),
            c_sv=np.ascontiguousarray(inp["cache_swa_v"][sl].reshape(NS * 8, 16 * 256)),
            c_mk=np.ascontiguousarray(inp["cache_mem_k"][:, sl].reshape(4, NS * 8, 32 * 256)),
            c_mv=np.ascontiguousarray(inp["cache_mem_v"][:, sl].reshape(4, NS * 8, 32 * 256)),
            wpack=wpack, wmem=wmem, cpk=cpk_full, rcos=rcos, rsin=rsin, rcs_s=rcs_s, rowc=rowc,
            ident=cm["ident"], rotm=cm["rotm"], sel=cm["sel"], ones_bf=cm["ones_bf"], blk_bf=cm["blk_bf"], mask_bf=cm["mask_bf"],
        )
        in_maps.append(m)
    res = run_bass_kernel_spmd(nc, in_maps, core_ids=list(range(ncores)))
    R = res.results
    per = ncores // B
    own = [b * per for b in range(B)]
    f = np.float32
    y_prompt = np.stack([R[c]["y_p"] for c in own]).astype(f)
    y_sample = np.concatenate([R[c]["y_s"] for c in range(ncores)])[:, None, :].astype(f)
    p_h = np.stack([R[c]["o_ph"] for c in own], axis=1).astype(f)
    p_rc = np.stack([R[c]["o_prc"] for c in own], axis=1).astype(f)
    p_fc = np.stack([R[c]["o_pfc"] for c in own], axis=1).astype(f)
    p_k = np.stack([R[c]["o_pk"].reshape(128, 4, 64) for c in own]).astype(f)
    p_v = np.stack([R[c]["o_pv"].reshape(128, 4, 64) for c in own]).astype(f)
    p_mk = np.stack([R[c]["o_pmk"].reshape(4, NMEM, 4, 64) for c in own], axis=1).astype(f)
    p_mv = np.stack([R[c]["o_pmv"].reshape(4, NMEM, 4, 64) for c in own], axis=1).astype(f)
    s_h = np.concatenate([R[c]["o_sh"] for c in range(ncores)], axis=1).astype(f)
    s_rc = np.concatenate([R[c]["o_src"].reshape(2, NS, 3, DRNN) for c in range(ncores)], axis=1).astype(f)
    s_fc = np.concatenate([R[c]["o_sfc"].reshape(4, NS, 2, 2 * DFF) for c in range(ncores)], axis=1).astype(f)
    s_k = np.concatenate([R[c]["o_sk"].reshape(NS, 1, 4, 64) for c in range(ncores)]).astype(f)
    s_v = np.concatenate([R[c]["o_sv"].reshape(NS, 1, 4, 64) for c in range(ncores)]).astype(f)
    return (y_prompt, y_sample, p_h, p_rc, p_fc, p_k, p_v, p_mk, p_mv, s_h, s_rc, s_fc, s_k, s_v)
```

```python
import contextlib
import os
import numpy as np
import ml_dtypes
import concourse.bass as bass
import concourse.mybir as mybir
from concourse.bass_utils import run_bass_kernel_spmd

F32 = mybir.dt.float32
BF16 = mybir.dt.bfloat16
AF = mybir.ActivationFunctionType
ALU = mybir.AluOpType
AX = mybir.AxisListType

D = 1024
DEPTH = 4
HD = 64
DRNN = 768
DFF = 3072
NMEM = 256
PAST = 8192
EPS = 1e-6
NS = 16
TC = 512
PIECE = 8192
SEM_ROT = 15000


class Tile:
    __slots__ = ("t", "w", "r", "dsem", "dcnt", "dw", "dr", "name")

    def __init__(self, t, name):
        self.t = t
        self.name = name
        self.w = {}
        self.r = {}
        self.dsem = {}
        self.dcnt = {}
        self.dw = {}
        self.dr = {}

    def __getitem__(self, k):
        return self.t[k]


class Eng:
    def __init__(self, name, obj):
        self.name = name
        self.obj = obj
        self.sems = []
        self.cnt = 0
        self.seen = {}
        self.seen_d = {}


class KB:
    def __init__(self, nc, es):
        self.nc = nc
        self.es = es
        self.E = {
            "pe": Eng("pe", nc.tensor), "act": Eng("act", nc.scalar), "dve": Eng("dve", nc.vector),
            "pool": Eng("pool", nc.gpsimd), "sp": Eng("sp", nc.sync),
        }
        self.nsem = 0
        self.tiles = []
        self.pools = {}
        self.psums = []
        self.psi = 0

    def sem(self, name):
        self.nsem += 1
        return self.es.enter_context(self.nc.semaphore(f"{name}_{self.nsem}"))

    def sb(self, name, shape, dtype, es=None):
        t = (es or self.es).enter_context(self.nc.sbuf_tensor("sb_" + name, list(shape), dtype))
        tl = Tile(t, name)
        self.tiles.append(tl)
        return tl

    def pseudo(self, name):
        tl = Tile(None, name)
        self.tiles.append(tl)
        return tl

    def init_psum(self):
        for i in range(8):
            t = self.es.enter_context(self.nc.psum_tensor(f"ps{i}", [128, 512], F32))
            self.psums.append(Tile(t, f"ps{i}"))

    def ps(self):
        t = self.psums[self.psi % 8]
        self.psi += 1
        return t

    def pool(self, name, n, shape, dtype, es=None):
        key = name
        if key not in self.pools:
            self.pools[key] = [[self.sb(f"{name}{i}", shape, dtype, es) for i in range(n)], 0]
        p = self.pools[key]
        t = p[0][p[1] % len(p[0])]
        p[1] += 1
        return t

    def _cur_sem(self, e):
        ep = e.cnt // SEM_ROT
        while len(e.sems) <= ep:
            e.sems.append(self.sem(e.name))
        return ep

    def _wait_eng(self, e, oname, val):
        if e.seen.get(oname, 0) >= val:
            return
        o = self.E[oname]
        ep = (val - 1) // SEM_ROT
        e.obj.wait_ge(o.sems[ep], val - ep * SEM_ROT)
        e.seen[oname] = val

    def _wait_dma(self, e, tile, q, val):
        if val <= 0:
            return
        k = (id(tile), q)
        if e.seen_d.get(k, 0) >= val:
            return
        e.obj.wait_ge(tile.dsem[q], val)
        e.seen_d[k] = val

    def _deps(self, e, reads, writes):
        for t in reads:
            for on, v in t.w.items():
                if not (on == "pe" and e.name == "pe"):
                    self._wait_eng(e, on, v)
            for q, v in t.dw.items():
                self._wait_dma(e, t, q, v)
        for t in writes:
            for on, v in t.w.items():
                if not (on == "pe" and e.name == "pe"):
                    self._wait_eng(e, on, v)
            for on, v in t.r.items():
                if not (on == "pe" and e.name == "pe"):
                    self._wait_eng(e, on, v)
            for q, v in t.dw.items():
                self._wait_dma(e, t, q, v)
            for q, v in t.dr.items():
                self._wait_dma(e, t, q, v)

    def op(self, en, fn, reads=(), writes=()):
        e = self.E[en]
        self._deps(e, reads, writes)
        ep = self._cur_sem(e)
        ins = fn()
        ins.then_inc(e.sems[ep], 1)
        e.cnt += 1
        e.seen[en] = max(e.seen.get(en, 0), 0)
        for t in writes:
            t.w[en] = e.cnt
        for t in reads:
            t.r[en] = e.cnt

    def dma(self, qn, out, in_, reads=(), writes=(), semtile=None, extra=()):
        e = self.E[qn]
        self._deps(e, reads, writes)
        for (tl, q, v) in extra:
            self._wait_dma(e, tl, q, v)
        st = semtile or (writes[0] if writes else reads[0])
        if qn not in st.dsem:
            st.dsem[qn] = self.sem("d")
            st.dcnt[qn] = 0
        ins = e.obj.dma_start(out=out, in_=in_)
        ins.then_inc(st.dsem[qn], 16)
        st.dcnt[qn] += 16
        assert st.dcnt[qn] < 30000, st.name
        for t in writes:
            assert t is st
            t.dw[qn] = st.dcnt[qn]
        for t in reads:
            assert t is st
            t.dr[qn] = st.dcnt[qn]
        return (st, qn, st.dcnt[qn])

    def barrier(self):
        for e in self.E.values():
            for on, o in self.E.items():
                if o.cnt > 0 and on != e.name:
                    self._wait_eng(e, on, o.cnt)
            for t in self.tiles:
                for q, v in t.dcnt.items():
                    self._wait_dma(e, t, q, v)

    def finish(self):
        e = self.E["sp"]
        for on, o in self.E.items():
            if o.cnt > 0 and on != "sp":
                self._wait_eng(e, on, o.cnt)
        for t in self.tiles:
            for q, v in t.dcnt.items():
                self._wait_dma(e, t, q, v)


def _kpack(w):
    K, C = w.shape
    return np.ascontiguousarray(w.reshape(K // 128, 128, C).transpose(1, 0, 2).reshape(128, -1))


def _pad_piece(a):
    out = np.zeros((128, PIECE), np.float32)
    out[:, : a.shape[1]] = a
    return out


def piece_names():
    names = []
    for l in range(2):
        names += [f"g{l}", f"ina{l}_0", f"ina{l}_1", f"out{l}"] + [f"up{l}_{i}" for i in range(6)] + [f"dn{l}_{i}" for i in range(3)]
    names += ["kv"]
    for l in range(2, 4):
        names += [f"inb{l}", f"out{l}"] + [f"up{l}_{i}" for i in range(6)] + [f"dn{l}_{i}" for i in range(3)]
    return names


def make_wpack(inp):
    P = {}
    for l in range(2):
        bd = np.zeros((128, 2 * 768), np.float32)
        for gi, nm in enumerate(["w_gate_x", "w_gate_a"]):
            w = inp[nm][l]
            for c in range(6):
                for i in range(2):
                    bd[64 * i:64 * i + 64, gi * 768 + c * 128 + 64 * i: gi * 768 + c * 128 + 64 * i + 64] = w[2 * c + i]
        P[f"g{l}"] = bd
        wi = inp["w_in_a"][l]
        P[f"ina{l}_0"] = _kpack(wi[:, :896])
        P[f"ina{l}_1"] = _kpack(wi[:, 896:])
    for l in range(4):
        P[f"out{l}"] = _kpack(inp["w_out"][l])
        wu = inp["w_ffn_up"][l]
        for i in range(6):
            P[f"up{l}_{i}"] = _kpack(np.concatenate([wu[:, i * 512:(i + 1) * 512], wu[:, DFF + i * 512: DFF + (i + 1) * 512]], axis=1))
        wd = inp["w_ffn_down"][l]
        for i in range(3):
            P[f"dn{l}_{i}"] = _kpack(wd[i * 1024:(i + 1) * 1024])
    for l in range(2, 4):
        P[f"inb{l}"] = _kpack(inp["w_in_b"][l - 2])
    wkv = inp["w_kv"]
    cols = []
    for kv in range(4):
        cols += [wkv[:, kv * 64:(kv + 1) * 64], wkv[:, kv * 64:(kv + 1) * 64]]
    cols.append(wkv[:, 256:])
    P["kv"] = _kpack(np.concatenate(cols, axis=1))
    names = piece_names()
    return np.stack([_pad_piece(P[n]) for n in names]), {n: i for i, n in enumerate(names)}


def _fm(v):
    return np.ascontiguousarray(v.reshape(-1, 128).T)


def make_cpk(inp):
    cols = []
    idx = {}

    def add(name, a):
        a = np.asarray(a, np.float32)
        if a.ndim == 1:
            a = a[:, None]
        idx[name] = sum(c.shape[1] for c in cols)
        cols.append(a)

    def head_vec(g):
        return np.concatenate([g, g])[:, None]

    for l in range(4):
        add(f"gmix{l}", _fm(inp["norm_mix_g"][l]))
        add(f"gffn{l}", _fm(inp["norm_ffn_g"][l]))
        for j in range(3):
            add(f"fcw{l}_{j}", _fm(inp["ffn_conv_w"][l, j]))
        add(f"fcb{l}", _fm(inp["ffn_conv_b"][l]))
        add(f"mqg{l}", head_vec(inp["mem_q_norm_g"][l]))
        add(f"memg{l}", _fm(inp["mem_norm_g"][l]))
    for l in range(2):
        for j in range(4):
            add(f"rcw{l}_{j}", _fm(inp["rnn_conv_w"][l, j]))
        add(f"rcb{l}", _fm(inp["rnn_conv_b"][l]))
        add(f"bgx{l}", _fm(inp["b_gate_x"][l]))
        add(f"bga{l}", _fm(inp["b_gate_a"][l]))
        add(f"lru{l}", _fm(inp["lru_param"][l]))
    for j in range(2):
        add(f"qg{j + 2}", head_vec(inp["q_norm_g"][j]))
    add("kvg", _fm(inp["kv_norm_g"]))
    add("kng", head_vec(inp["k_norm_g"]))
    return np.ascontiguousarray(np.concatenate(cols, axis=1)), idx


def rope_tables(pos):
    half = HD // 2
    inv = (10000.0 ** (-np.arange(half, dtype=np.float32) / half)).astype(np.float32)
    ang = pos.astype(np.float32)[None, :] * inv[:, None]
    c = np.cos(ang).astype(np.float32)
    s = np.sin(ang).astype(np.float32)
    return np.ascontiguousarray(np.tile(c, (4, 1))), np.ascontiguousarray(np.tile(s, (4, 1)))


def const_mats():
    ident = np.eye(128, dtype=np.float32)
    rot = np.zeros((128, 128), np.float32)
    for b in range(2):
        for d in range(32):
            rot[64 * b + d + 32, 64 * b + d] = -1.0
            rot[64 * b + d, 64 * b + d + 32] = 1.0
    ones = np.ones((128, 128), np.float32)
    blk = np.zeros((128, 128), np.float32)
    blk[:64, :64] = 1.0
    blk[64:, 64:] = 1.0
    l = np.arange(128)[:, None]
    t = np.arange(128)[None, :]
    tri_cur = (l <= t).astype(np.float32)
    tri_prev = (l >= t).astype(np.float32)
    mask = np.concatenate([tri_cur, tri_prev], axis=1)
    sel = (np.arange(128)[:, None] // 8 == np.arange(16)[None, :]).astype(np.float32)
    bf = ml_dtypes.bfloat16
    return dict(ident=ident, rotm=rot, ones_bf=ones.astype(bf), blk_bf=blk.astype(bf), mask_bf=mask.astype(bf), sel=sel)


def build(SEQ, split=False):
    NCH = SEQ // TC
    NR = NCH // 2 - 1 if split else 0
    NF = NCH - NR - (NCH // 2 - 1 if split else 0) if not split else NCH // 2 + 1
    SEQF = NF * TC
    names = piece_names()
    pidx = {n: i for i, n in enumerate(names)}
    NP = len(names)
    nc = bass.Bass("TRN2", target_bir_lowering=False)

    def din(name, shape, dt=F32):
        return nc.dram_tensor(name, list(shape), dt, kind="ExternalInput").ap()

    def dout(name, shape):
        return nc.dram_tensor(name, list(shape), F32, kind="ExternalOutput").ap()

    xp = din("xp", [SEQ, D]); xs = din("xs", [NS, D]); memp = din("memp", [NMEM, D])
    st_h = din("st_h", [2, NS, DRNN]); st_rc = din("st_rc", [2, NS, 3 * DRNN]); st_fc = din("st_fc", [4, NS, 2 * 2 * DFF])
    c_sk = din("c_sk", [NS * 8, 16 * 256]); c_sv = din("c_sv", [NS * 8, 16 * 256])
    c_mk = din("c_mk", [4, NS * 8, 32 * 256]); c_mv = din("c_mv", [4, NS * 8, 32 * 256])
    wpack = din("wpack", [NP, 128, PIECE]); wmem = din("wmem", [4, 128, 8 * 512])
    cpk_np_cols = None
    cpk_d = din("cpk", [128, 1000])
    rcos = din("rcos", [128, SEQF]); rsin = din("rsin", [128, SEQF])
    flag_d = din("flag", [128, 1])
    rcs_s = din("rcs_s", [128, 2])
    rowc = din("rowc", [4 * 256 + 24])
    ident_d = din("ident", [128, 128]); rotm_d = din("rotm", [128, 128]); sel_d = din("sel", [128, 16])
    ones_d = din("ones_bf", [128, 128], BF16); blk_d = din("blk_bf", [128, 128], BF16); mask_d = din("mask_bf", [128, 256], BF16)
    y_p = dout("y_p", [SEQF, D]); y_s = dout("y_s", [NS, D])
    o_ph = dout("o_ph", [2, DRNN]); o_prc = dout("o_prc", [2, 3, DRNN]); o_pfc = dout("o_pfc", [4, 2, 2 * DFF])
    o_pk = dout("o_pk", [128, 256]); o_pv = dout("o_pv", [128, 256])
    o_pmk = dout("o_pmk", [4, NMEM, 256]); o_pmv = dout("o_pmv", [4, NMEM, 256])
    o_sh = dout("o_sh", [2, NS, DRNN]); o_src = dout("o_src", [2, NS, 3 * DRNN]); o_sfc = dout("o_sfc", [4, NS, 2 * 2 * DFF])
    o_sk = dout("o_sk", [NS, 256]); o_sv = dout("o_sv", [NS, 256])
    wbf = nc.dram_tensor("wbf", [NP, 128, PIECE], BF16, kind="Internal").ap()

    with contextlib.ExitStack() as es:
        K = KB(nc, es)
        K.init_psum()
        op = K.op
        CI = build.cidx

        slots = [K.sb(f"slot{i}", [128, PIECE], BF16) for i in range(4)]
        cpk = K.sb("cpk", [128, 1000], F32)
        ident = K.sb("ident", [128, 128], F32); rotm = K.sb("rotm", [128, 128], F32); sel = K.sb("sel", [128, 16], F32)
        ones_bf = K.sb("ones_bf", [128, 128], BF16); blk_bf = K.sb("blk_bf", [128, 128], BF16); mask_bf = K.sb("mask_bf", [128, 256], BF16)
        rowc_t = K.sb("rowc", [128, 24], F32)
        esink = K.sb("esink", [128, 24], F32)
        clt = K.sb("clt", [128, 12], F32); hclt = K.sb("hclt", [128, 12], F32)
        hbx = K.sb("hbx", [128, 12], F32); hba = K.sb("hba", [128, 12], F32); lnhalf = K.sb("lnhalf", [128, 1], F32)
        rcs_t = K.sb("rcs_s", [128, 2], F32)
        flag_t = K.sb("flag", [128, 1], F32)
        K.dma("sp", flag_t[:], flag_d[:, :], writes=[flag_t])
        for tl, src in [(cpk, cpk_d), (ident, ident_d), (rotm, rotm_d), (sel, sel_d), (ones_bf, ones_d), (blk_bf, blk_d), (mask_bf, mask_d), (rcs_t, rcs_s)]:
            K.dma("sp", tl[:], src[:, :], writes=[tl])
        K.dma("sp", rowc_t[:], rowc[1024:1048].partition_broadcast(128), writes=[rowc_t])

        def cc(name, j=0, n=1):
            b = CI[name] + j
            return cpk[:, b:b + n]

        op("act", lambda: nc.scalar.activation(out=esink[:], in_=rowc_t[:, 0:24], func=AF.Exp), [rowc_t], [esink])
        for l in range(2):
            tmp = K.pool("f", 8, [128, 516], F32)
            op("act", lambda: nc.scalar.activation(out=tmp[:, 0:6], in_=cc(f"lru{l}", 0, 6), func=AF.Exp, scale=-1.0), [cpk], [tmp])
            op("act", lambda: nc.scalar.activation(out=tmp[:, 8:14], in_=tmp[:, 0:6], func=AF.Ln, bias=1.0), [tmp], [tmp])
            op("pool", lambda: nc.gpsimd.tensor_scalar(out=clt[:, l * 6:l * 6 + 6], in0=tmp[:, 8:14], scalar1=-8.0, scalar2=None, op0=ALU.mult), [tmp], [clt])
            op("pool", lambda: nc.gpsimd.tensor_scalar(out=hclt[:, l * 6:l * 6 + 6], in0=tmp[:, 8:14], scalar1=-4.0, scalar2=None, op0=ALU.mult), [tmp], [hclt])
            op("pool", lambda: nc.gpsimd.tensor_scalar(out=hbx[:, l * 6:l * 6 + 6], in0=cc(f"bgx{l}", 0, 6), scalar1=0.5, scalar2=None, op0=ALU.mult), [cpk], [hbx])
            op("pool", lambda: nc.gpsimd.tensor_scalar(out=hba[:, l * 6:l * 6 + 6], in0=cc(f"bga{l}", 0, 6), scalar1=0.5, scalar2=None, op0=ALU.mult), [cpk], [hba])
        op("pool", lambda: nc.gpsimd.memset(lnhalf[:, :], -0.6931471805599453), [], [lnhalf])

        K.pool("f", 8, [128, 516], F32); K.pool("b", 12, [128, 512], BF16); K.pool("pm", 12, [128, 256], BF16); K.pool("fa", 4, [128, 516], F32)
        for nm in ("f", "b", "pm", "fa"):
            K.pools[nm][1] = 0
        WS = {"n": 0, "loaded": 0, "store": {}, "done": 0}
        stream = []

        def ws_issue(upto):
            while WS["loaded"] < min(upto, len(stream)):
                g = WS["loaded"]
                pid, p0 = stream[g]
                sl = slots[g % 4]
                if p0:
                    K.dma("pool", sl[:], wpack[pid], writes=[sl])
                    WS["store"][pid] = K.dma("sp", wbf[pid], sl[:], reads=[sl])
                else:
                    K.dma("sp", sl[:], wbf[pid], writes=[sl], extra=[WS["store"][pid]])
                WS["loaded"] += 1

        def ws_get(name, hold=False):
            g = WS["n"]
            assert stream[g][0] == pidx[name], (name, names[stream[g][0]])
            if not hold:
                WS["done"] = g
            ws_issue(min(g + 4, WS["done"] + 4))
            assert WS["loaded"] > g
            WS["n"] += 1
            return slots[g % 4]

        casted = set()
        names_red = [n for n in names if n.endswith("0") and not n.startswith("inb") or n in ("ina0_1", "g1", "ina1_0", "ina1_1")]
        names_red = [f"g0", "ina0_0", "ina0_1", "out0"] + [f"up0_{i}" for i in range(6)] + [f"dn0_{i}" for i in range(3)] + ["g1", "ina1_0", "ina1_1"]

        def add_pass(subset=None):
            for n in (subset or names):
                stream.append((pidx[n], pidx[n] not in casted))
                casted.add(pidx[n])

        def fill(n, N=512):
            return

        def interleave(gens, width=2):
            active = []
            gens = list(gens)
            while gens or active:
                while gens and len(active) < width:
                    active.append(gens.pop(0))
                for g in list(active):
                    try:
                        next(g)
                    except StopIteration:
                        active.remove(g)

        def fp():
            return K.pool("f", 8, [128, 516], F32)

        def bp():
            return K.pool("b", 12, [128, 512], BF16)

        def rmsnorm(X, gname, out, N):
            ss = K.ps()
            for c in range(8):
                sq = bp()
                if c % 2 == 0 or N != TC:
                    op("act", lambda: nc.scalar.activation(out=sq[:, :N], in_=X[c][:, :N], func=AF.Square), [X[c]], [sq])
                else:
                    op("pool", lambda: nc.gpsimd.tensor_tensor(out=sq[:, :N], in0=X[c][:, :N], in1=X[c][:, :N], op=ALU.mult), [X[c]], [sq])
                op("pe", lambda: nc.tensor.matmul(ss[:, :N], ones_bf[:], sq[:, :N], start=(c == 0), stop=(c == 7)), [sq, ones_bf], [ss])
            fill(20, N)
            rb = fp()
            op("act", lambda: nc.scalar.activation(out=rb[:, :N], in_=ss[:, :N], func=AF.Ln, scale=1.0 / D, bias=EPS), [ss], [rb])
            op("act", lambda: nc.scalar.activation(out=rb[:, :N], in_=rb[:, :N], func=AF.Exp, scale=-0.5), [rb], [rb])
            for c in range(8):
                op("dve", lambda: nc.vector.scalar_tensor_tensor(out=out[c][:, :N], in0=X[c][:, :N], scalar=cc(gname, c), in1=rb[:, :N], op0=ALU.mult, op1=ALU.mult), [X[c], rb, cpk], [out[c]])

        def proj(slot, stride, col, hn, N):
            pt = K.ps()
            for k in range(8):
                op("pe", lambda: nc.tensor.matmul(pt[:, :N], slot[:, k * stride + col:k * stride + col + 128], hn[k][:, :N], start=(k == 0), stop=(k == 7)), [slot, hn[k]], [pt])
            return pt

        def headnorm(pq, gcol, N, out_t, out_dt_bf16):
            for _ in headnorm_g(pq, gcol, N, out_t):
                pass

        def headnorm_g(pq, gcol, N, out_t):
            sq = bp()
            op("act", lambda: nc.scalar.activation(out=sq[:, :N], in_=pq[:, :N], func=AF.Square), [pq], [sq])
            yield
            fill(4, N)
            ssh = K.ps()
            op("pe", lambda: nc.tensor.matmul(ssh[:, :N], blk_bf[:], sq[:, :N], start=True, stop=True), [sq, blk_bf], [ssh])
            r = fp()
            op("act", lambda: nc.scalar.activation(out=r[:, :N], in_=ssh[:, :N], func=AF.Ln, scale=1.0 / HD, bias=EPS), [ssh], [r])
            op("act", lambda: nc.scalar.activation(out=r[:, :N], in_=r[:, :N], func=AF.Exp, scale=-0.5), [r], [r])
            op("dve", lambda: nc.vector.scalar_tensor_tensor(out=out_t[:, :N], in0=pq[:, :N], scalar=gcol, in1=r[:, :N], op0=ALU.mult, op1=ALU.mult), [pq, r, cpk], [out_t])
            yield

        def rope(qn, cos_ap, sin_ap, cs_tiles, out_ap, out_tile, N, per_part):
            fill(5, N)
            pr = K.ps()
            op("pe", lambda: nc.tensor.matmul(pr[:, :N], rotm[:], qn[:, :N], start=True, stop=True), [rotm, qn], [pr])
            t1 = fp(); t2 = fp()
            if per_part:
                op("dve", lambda: nc.vector.tensor_scalar(out=t1[:, :N], in0=qn[:, :N], scalar1=cos_ap, scalar2=None, op0=ALU.mult), [qn] + cs_tiles, [t1])
                op("dve", lambda: nc.vector.tensor_scalar(out=t2[:, :N], in0=pr[:, :N], scalar1=sin_ap, scalar2=None, op0=ALU.mult), [pr] + cs_tiles, [t2])
            else:
                op("dve", lambda: nc.vector.tensor_tensor(out=t1[:, :N], in0=qn[:, :N], in1=cos_ap, op=ALU.mult), [qn] + cs_tiles, [t1])
                op("dve", lambda: nc.vector.tensor_tensor(out=t2[:, :N], in0=pr[:, :N], in1=sin_ap, op=ALU.mult), [pr] + cs_tiles, [t2])
            rope.cnt = getattr(rope, "cnt", 0) + 1
            if rope.cnt % 2 == 0 or N != TC:
                op("pool", lambda: nc.gpsimd.tensor_tensor(out=out_ap, in0=t1[:, :N], in1=t2[:, :N], op=ALU.add), [t1, t2], [out_tile])
            else:
                op("dve", lambda: nc.vector.tensor_tensor(out=out_ap, in0=t1[:, :N], in1=t2[:, :N], op=ALU.add), [t1, t2], [out_tile])
            return t1, t2

        def transpose_to(pt, col, in_ap, in_tile, kparts):
            op("pe", lambda: nc.tensor.transpose(out=pt[:in_ap.shape[1], col:col + kparts], in_=in_ap, identity=ident[0:kparts, 0:kparts]), [in_tile, ident], [pt])

        def ffn(l, X, hn, act_t, N, sample, S):
            KFFN = int(os.environ.get("KFFN", "99")) if (l == 1 and not sample) else 99
            for i in range(6):
                if i >= KFFN:
                    return
                slot = ws_get(f"up{l}_{i}")
                for jj in range(4):
                    j = 4 * i + jj
                    accs = []
                    for half in range(2):
                        cidx = j + 24 * half
                        pu = proj(slot, 1024, half * 512 + jj * 128, hn, N)
                        acc = fp()
                        op("act", lambda: nc.scalar.activation(out=acc[:, :N], in_=pu[:, :N], func=AF.Identity, scale=cc(f"fcw{l}_2", cidx), bias=cc(f"fcb{l}", cidx)), [pu, cpk], [acc])
                        if not sample:
                            hv = S["HIST"][l][:].rearrange("p (j c) -> p j c", c=48)[:, :, cidx]
                            ub = fp()
                            op("pool", lambda: nc.gpsimd.tensor_copy(out=ub[:, 0:2], in_=hv), [S["HIST"][l]], [ub])
                            op("act", lambda: nc.scalar.copy(out=ub[:, 2:2 + N], in_=pu[:, :N]), [pu], [ub])
                            op("pool", lambda: nc.gpsimd.tensor_copy(out=hv, in_=ub[:, N:N + 2]), [ub], [S["HIST"][l]])
                            taps = [(ub[:, 1:1 + N], [ub]), (ub[:, 0:N], [ub])]
                        else:
                            fst = S["FST"]
                            fv = fst[:].rearrange("p (j c s) -> p j c s", j=2, c=48)
                            op("act", lambda: nc.scalar.copy(out=S["UNEW"][:].rearrange("p (c s) -> p c s", c=48)[:, cidx, :], in_=pu[:, :N]), [pu], [S["UNEW"]])
                            taps = [(fv[:, 1, cidx, :], [fst]), (fv[:, 0, cidx, :], [fst])]
                        for ti, (tap, trd) in enumerate(taps):
                            wname = f"fcw{l}_{1 - ti}"
                            op("dve", lambda: nc.vector.scalar_tensor_tensor(out=acc[:, :N], in0=tap, scalar=cc(wname, cidx), in1=acc[:, :N], op0=ALU.mult, op1=ALU.add), trd + [acc, cpk], [acc])
                        accs.append(acc)
                    gel = fp()
                    op("act", lambda: nc.scalar.activation(out=gel[:, :N], in_=accs[0][:, :N], func=AF.Gelu_apprx_tanh), [accs[0]], [gel])
                    op("dve", lambda: nc.vector.tensor_tensor(out=act_t[j][:, :N], in0=gel[:, :N], in1=accs[1][:, :N], op=ALU.mult), [gel, accs[1]], [act_t[j]])
            if KFFN == 98:
                return
            pds = [K.ps() for _ in range(8)]
            for i in range(3):
                slot = ws_get(f"dn{l}_{i}")
                for c in range(8):
                    for kk in range(8):
                        op("pe", lambda: nc.tensor.matmul(pds[c][:, :N], slot[:, kk * 1024 + c * 128: kk * 1024 + c * 128 + 128], act_t[8 * i + kk][:, :N], start=(i == 0 and kk == 0), stop=(i == 2 and kk == 7)), [slot, act_t[8 * i + kk]], [pds[c]])
            for c in range(8):
                op("dve", lambda: nc.vector.tensor_tensor(out=X[c][:, :N], in0=X[c][:, :N], in1=pds[c][:, :N], op=ALU.add), [X[c], pds[c]], [X[c]])

        def mem_q(pq, l, N):
            qn = K.pool("fa", 4, [128, 516], F32)
            headnorm(pq, cc(f"mqg{l}"), N, qn, False)
            return qn

        def mem_attn_prompt(l, qns, main, N, S):
            interleave([mem_head(l, qns, main, N, S, h) for h in range(4)], 2)

        def mem_head(l, qns, main, N, S, h):
            if True:
                c = h // 2; o = (h % 2) * 64
                qb = bp()
                op("pool", lambda: nc.gpsimd.tensor_copy(out=qb[o:o + 64, :N], in_=qns[c][o:o + 64, :N]), [qns[c]], [qb])
                pts = []
                for mb in range(2):
                    sT = K.ps()
                    op("pe", lambda: nc.tensor.matmul(sT[:, :N], S["kmT"][l][c][o:o + 64, mb * 128:(mb + 1) * 128], qb[o:o + 64, :N], start=True, stop=True), [S["kmT"][l][c], qb], [sT])
                    pT = bp()
                    op("act", lambda: nc.scalar.activation(out=pT[:, :N], in_=sT[:, :N], func=AF.Exp, scale=0.125), [sT], [pT])
                    pts.append(pT)
                yield
                fill(4, N)
                ops_ = K.ps(); dps = K.ps()
                for mb in range(2):
                    op("pe", lambda: nc.tensor.matmul(ops_[:, :N], S["vm"][l][mb][:, c * 128:(c + 1) * 128], pts[mb][:, :N], start=(mb == 0), stop=(mb == 1)), [S["vm"][l][mb], pts[mb]], [ops_])
                for mb in range(2):
                    op("pe", lambda: nc.tensor.matmul(dps[:, :N], ones_bf[:], pts[mb][:, :N], start=(mb == 0), stop=(mb == 1)), [ones_bf, pts[mb]], [dps])
                rd = fp()
                op("act", lambda: nc.scalar.activation(out=rd[o:o + 64, :N], in_=dps[o:o + 64, :N], func=AF.Ln), [dps], [rd])
                op("act", lambda: nc.scalar.activation(out=rd[o:o + 64, :N], in_=rd[o:o + 64, :N], func=AF.Exp, scale=-1.0), [rd], [rd])
                op("dve", lambda: nc.vector.tensor_tensor(out=main[6 + c][o:o + 64, :N], in0=ops_[o:o + 64, :N], in1=rd[o:o + 64, :N], op=ALU.mult), [ops_, rd], [main[6 + c]])

        def sample_attn(Ksrc, Vsrc, Lp, qfm, nq, group, S, main, main0, extra=None):
            F = nq * 64
            nch = F // 128
            W = Lp * 256
            A = S["BUFA"]; T = S["BUFT"] if group > 1 else S["BUFA"]
            qrep = S["QREP"]
            K.dma("sp", A[:, :W], Ksrc, writes=[A])
            for c in range(nch):
                qx = fp()
                op("dve", lambda: nc.vector.tensor_copy(out=qx[:, 0:128].rearrange("p (s m) -> p s m", m=8), in_=qfm[c][:, 0:16].unsqueeze(2).to_broadcast([128, 16, 8])), [qfm[c]], [qx])
                pt = K.ps()
                transpose_to(pt, 0, qx[:, 0:128], qx, 128)
                op("act", lambda: nc.scalar.copy(out=qrep[:, c * 128:(c + 1) * 128], in_=pt[:, 0:128]), [pt], [qrep])
            SC = S["SC"]; PP = S["PP"]; DENP = S["DENP"]; OP = S["OPART"]
            qv = qrep[:, :F].rearrange("p (k g d) -> p k g d", k=4, g=group)
            for g in range(group):
                op("dve", lambda: nc.vector.tensor_tensor(out=T[:, :W].rearrange("p (l k d) -> p l k d", k=4, d=64), in0=A[:, :W].rearrange("p (l k d) -> p l k d", k=4, d=64), in1=qv[:, :, g, :].unsqueeze(1).to_broadcast([128, Lp, 4, 64]), op=ALU.mult), [A, qrep], [T])
                op("dve", lambda: nc.vector.tensor_reduce(out=SC[:, g * Lp * 4:(g + 1) * Lp * 4], in_=T[:, :W].rearrange("p (a d) -> p a d", d=64), axis=AX.X, op=ALU.add), [T], [SC])
            GL = group * Lp * 4
            op("act", lambda: nc.scalar.activation(out=PP[:, :GL], in_=SC[:, :GL], func=AF.Exp, scale=0.125), [SC], [PP])
            for g in range(group):
                op("dve", lambda: nc.vector.tensor_reduce(out=DENP[:, :nq].rearrange("p (k g) -> p k g", g=group)[:, :, g], in_=PP[:, g * Lp * 4:(g + 1) * Lp * 4].rearrange("p (l k) -> p k l", k=4), axis=AX.X, op=ALU.add), [PP], [DENP])
            K.dma("sp", A[:, :W], Vsrc, writes=[A])
            ov = OP[:, :F].rearrange("p (k g d) -> p k g d", k=4, g=group)
            for g in range(group):
                op("dve", lambda: nc.vector.tensor_tensor(out=T[:, :W].rearrange("p (l k d) -> p l k d", k=4, d=64), in0=A[:, :W].rearrange("p (l k d) -> p l k d", k=4, d=64), in1=PP[:, g * Lp * 4:(g + 1) * Lp * 4].rearrange("p (l k) -> p l k", k=4).unsqueeze(3).to_broadcast([128, Lp, 4, 64]), op=ALU.mult), [A, PP], [T])
                op("dve", lambda: nc.vector.tensor_reduce(out=ov[:, :, g, :], in_=T[:, :W].rearrange("p (l k d) -> p k d l", k=4, d=64), axis=AX.X, op=ALU.add), [T], [OP])
            OT = S["OT"]; DT = S["DT"]
            for f0 in range(0, F, 512):
                fw = min(512, F - f0)
                pt = K.ps()
                op("pe", lambda: nc.tensor.matmul(pt[0:16, :fw], sel[:, :], OP[:, f0:f0 + fw], start=True, stop=True), [sel, OP], [pt])
                op("act", lambda: nc.scalar.copy(out=OT[0:16, f0:f0 + fw], in_=pt[0:16, :fw]), [pt], [OT])
            pt = K.ps()
            op("pe", lambda: nc.tensor.matmul(pt[0:16, :nq], sel[:, :], DENP[:, :nq], start=True, stop=True), [sel, DENP], [pt])
            op("act", lambda: nc.scalar.copy(out=DT[0:16, :nq], in_=pt[0:16, :nq]), [pt], [DT])
            if extra is not None:
                extra(OT, DT)
            op("dve", lambda: nc.vector.reciprocal(out=DT[0:16, :nq], in_=DT[0:16, :nq]), [DT], [DT])
            op("dve", lambda: nc.vector.tensor_tensor(out=OT[0:16, :F].rearrange("p (n d) -> p n d", d=64), in0=OT[0:16, :F].rearrange("p (n d) -> p n d", d=64), in1=DT[0:16, :nq].unsqueeze(2).to_broadcast([16, nq, 64]), op=ALU.mult), [OT, DT], [OT])
            for c in range(nch):
                pt = K.ps()
                transpose_to(pt, 0, OT[0:16, c * 128:(c + 1) * 128], OT, 16)
                op("act", lambda: nc.scalar.copy(out=main[main0 + c][:, 0:16], in_=pt[:, 0:16]), [pt], [main[main0 + c]])

        def mixer_a(l, X, hn, main, N, sample, S, reduced=False):
            gs = ws_get(f"g{l}"); s0 = ws_get(f"ina{l}_0", True); s1 = ws_get(f"ina{l}_1", True)

            def inproj(j):
                sl = s0 if j < 7 else s1
                return proj(sl, 896, (j % 7) * 128, hn, N)

            def stage1(c):
                gact = K.pool("fa", 4, [128, 516], F32)
                if not reduced:
                    pg = inproj(c)
                    op("act", lambda: nc.scalar.activation(out=gact[:, :N], in_=pg[:, :N], func=AF.Gelu_apprx_tanh), [pg], [gact])
                px = inproj(6 + c)
                acc = K.pool("fa", 4, [128, 516], F32)
                if sample:
                    op("act", lambda: nc.scalar.activation(out=acc[:, :N], in_=px[:, :N], func=AF.Identity, scale=cc(f"rcw{l}_3", c), bias=cc(f"rcb{l}", c)), [px, cpk], [acc])
                else:
                    op("dve", lambda: nc.vector.tensor_scalar(out=acc[:, :N], in0=px[:, :N], scalar1=cc(f"rcw{l}_3", c), scalar2=cc(f"rcb{l}", c), op0=ALU.mult, op1=ALU.add), [px, cpk], [acc])
                if not sample:
                    RH = S["RH"][l]
                    hv = RH[:].rearrange("p (j c) -> p j c", c=6)[:, :, c]
                    xb = fp()
                    op("pool", lambda: nc.gpsimd.tensor_copy(out=xb[:, 0:3], in_=hv), [RH], [xb])
                    op("dve", lambda: nc.vector.tensor_copy(out=xb[:, 3:3 + N], in_=px[:, :N]), [px], [xb])
                    op("pool", lambda: nc.gpsimd.tensor_copy(out=hv, in_=xb[:, N:N + 3]), [xb], [RH])
                    taps = [(xb[:, 2:2 + N], [xb]), (xb[:, 1:1 + N], [xb]), (xb[:, 0:N], [xb])]
                else:
                    rst = S["RST"][l]
                    rv = rst[:].rearrange("p (j c s) -> p j c s", j=3, c=6)
                    op("act", lambda: nc.scalar.copy(out=S["XRN"][l][:].rearrange("p (c s) -> p c s", c=6)[:, c, :], in_=px[:, :N]), [px], [S["XRN"][l]])
                    taps = [(rv[:, 2, c, :], [rst]), (rv[:, 1, c, :], [rst]), (rv[:, 0, c, :], [rst])]
                for ti, (tap, trd) in enumerate(taps):
                    wname = f"rcw{l}_{2 - ti}"
                    op("dve", lambda: nc.vector.scalar_tensor_tensor(out=acc[:, :N], in0=tap, scalar=cc(wname, c), in1=acc[:, :N], op0=ALU.mult, op1=ALU.add), trd + [acc, cpk], [acc])
                xcb = bp()
                op("pool", lambda: nc.gpsimd.tensor_copy(out=xcb[:, :N], in_=acc[:, :N]), [acc], [xcb])
                fill(10, N)
                pgx = K.ps(); pga = K.ps()
                op("pe", lambda: nc.tensor.matmul(pgx[:, :N], gs[:, c * 128:(c + 1) * 128], xcb[:, :N], start=True, stop=True), [gs, xcb], [pgx])
                op("pe", lambda: nc.tensor.matmul(pga[:, :N], gs[:, 768 + c * 128:768 + (c + 1) * 128], xcb[:, :N], start=True, stop=True), [gs, xcb], [pga])
                return gact, acc, pgx, pga

            def stage2(c, gact, acc, pgx, pga):
                k = l * 6 + c
                gx = fp(); ga = fp(); aa = fp(); a2 = fp()
                op("act", lambda: nc.scalar.activation(out=gx[:, :N], in_=pgx[:, :N], func=AF.Tanh, scale=0.5, bias=hbx[:, k:k + 1]), [pgx, hbx], [gx])
                op("act", lambda: nc.scalar.activation(out=ga[:, :N], in_=pga[:, :N], func=AF.Tanh, scale=0.5, bias=hba[:, k:k + 1]), [pga, hba], [ga])
                op("act", lambda: nc.scalar.activation(out=aa[:, :N], in_=ga[:, :N], func=AF.Exp, scale=hclt[:, k:k + 1], bias=hclt[:, k:k + 1]), [ga, hclt], [aa])
                op("act", lambda: nc.scalar.activation(out=a2[:, :N], in_=ga[:, :N], func=AF.Exp, scale=clt[:, k:k + 1], bias=clt[:, k:k + 1]), [ga, clt], [a2])
                op("act", lambda: nc.scalar.activation(out=a2[:, :N], in_=a2[:, :N], func=AF.Ln, scale=-1.0, bias=1.0), [a2], [a2])
                op("act", lambda: nc.scalar.activation(out=a2[:, :N], in_=a2[:, :N], func=AF.Exp, scale=0.5, bias=lnhalf[:, 0:1]), [a2, lnhalf], [a2])
                op("dve", lambda: nc.vector.scalar_tensor_tensor(out=gx[:, :N], in0=gx[:, :N], scalar=1.0, in1=acc[:, :N], op0=ALU.add, op1=ALU.mult), [gx, acc], [gx])
                op("dve", lambda: nc.vector.tensor_tensor(out=gx[:, :N], in0=gx[:, :N], in1=a2[:, :N], op=ALU.mult), [gx, a2], [gx])
                hh = ga
                if not sample:
                    HS = S["HS"][l]
                    op("dve", lambda: nc.vector.tensor_tensor_scan(out=hh[:, :N], data0=aa[:, :N], data1=gx[:, :N], initial=HS[:, c:c + 1], op0=ALU.mult, op1=ALU.add), [aa, gx, HS], [hh])
                    op("pool", lambda: nc.gpsimd.tensor_copy(out=HS[:, c:c + 1], in_=hh[:, N - 1:N]), [hh], [HS])
                else:
                    h0 = S["H0"][l][:].rearrange("p (c s) -> p c s", c=6)[:, c, :]
                    op("dve", lambda: nc.vector.tensor_tensor(out=hh[:, :N], in0=aa[:, :N], in1=h0, op=ALU.mult), [aa, S["H0"][l]], [hh])
                    op("dve", lambda: nc.vector.tensor_tensor(out=hh[:, :N], in0=hh[:, :N], in1=gx[:, :N], op=ALU.add), [hh, gx], [hh])
                    op("pool", lambda: nc.gpsimd.tensor_copy(out=S["HNEW"][l][:].rearrange("p (c s) -> p c s", c=6)[:, c, :], in_=hh[:, :N]), [hh], [S["HNEW"][l]])
                if not reduced:
                    op("dve", lambda: nc.vector.tensor_tensor(out=main[c][:, :N], in0=hh[:, :N], in1=gact[:, :N], op=ALU.mult), [hh, gact], [main[c]])

            prev = stage1(0)
            for c in range(6):
                nxt = stage1(c + 1) if c + 1 < 6 else None
                stage2(c, *prev)
                prev = nxt
            qns = []
            if reduced:
                return None
            for c in range(2):
                pq = inproj(12 + c)
                qns.append(mem_q(pq, l, N))
            return qns

        def out_proj(l, X, main, N):
            slot = ws_get(f"out{l}")
            for c in range(8):
                pt = proj(slot, 1024, c * 128, main, N)
                op("dve", lambda: nc.vector.tensor_tensor(out=X[c][:, :N], in0=X[c][:, :N], in1=pt[:, :N], op=ALU.add), [X[c], pt], [X[c]])

        def shared_kv(X, hn, N, sample, S, ci):
            rmsnorm(X, "kvg", hn, N)
            slot = ws_get("kv")
            if not sample:
                kT = S["kT"]; Vt = S["Vtok"]
                if ci > 0:
                    for kv in range(4):
                        op("pool", lambda: nc.gpsimd.tensor_copy(out=kT[kv][:, 0:128], in_=kT[kv][:, 512:640]), [kT[kv]], [kT[kv]])
                    op("pool", lambda: nc.gpsimd.tensor_copy(out=Vt[0][:, :], in_=Vt[4][:, :]), [Vt[4]], [Vt[0]])
            KKV = int(os.environ.get("KKV", "9")) if not sample else 9
            for kv in range(4):
                if KKV < 2:
                    break
                pk = proj(slot, 768, kv * 128, hn, N)
                kn = fp()
                headnorm(pk, cc("kng"), N, kn, False)
                if KKV < 3:
                    continue
                if not sample:
                    t1, t2 = rope(kn, S["cos"][:, :N], S["sin"][:, :N], [S["cos"], S["sin"]], S["kT"][kv][:, 128:128 + N], S["kT"][kv], N, False)
                    if ci == NF - 1:
                        op("pool", lambda: nc.gpsimd.tensor_tensor(out=S["KF"][:, kv * 128:(kv + 1) * 128], in0=t1[:, N - 128:N], in1=t2[:, N - 128:N], op=ALU.add), [t1, t2], [S["KF"]])
                else:
                    rope(kn, rcs_t[:, 0:1], rcs_t[:, 1:2], [rcs_t], S["KNEW"][:, kv * 16:(kv + 1) * 16], S["KNEW"], N, True)
            if not sample:
                pass
            if KKV < 4:
                return
            nb = max(1, N // 128)
            for tb in range(nb):
                w = min(128, N)
                pv = K.ps()
                for k in range(8):
                    op("pe", lambda: nc.tensor.matmul(pv[0:w, 0:256], hn[k][:, tb * 128:tb * 128 + w], slot[:, k * 768 + 512:k * 768 + 768], start=(k == 0), stop=(k == 7)), [hn[k], slot], [pv])
                if not sample:
                    op("act", lambda: nc.scalar.copy(out=S["Vtok"][tb + 1][:, 64:320], in_=pv[:, 0:256]), [pv], [S["Vtok"][tb + 1]])
                    if ci == NF - 1 and tb == nb - 1:
                        op("act", lambda: nc.scalar.copy(out=S["VF"][:, :], in_=pv[:, 0:256]), [pv], [S["VF"]])
                else:
                    op("act", lambda: nc.scalar.copy(out=S["VNEW"][0:16, :], in_=pv[0:16, 0:256]), [pv], [S["VNEW"]])

        def mixer_b_q(l, hn, N, sample, S):
            slot = ws_get(f"inb{l}")
            qts = [None] * 6
            qns = [None] * 2

            def qchain(c):
                pq = proj(slot, 1024, c * 128, hn, N)
                qn = fp()
                yield from headnorm_g(pq, cc(f"qg{l}"), N, qn)
                if not sample:
                    qt = S["qT"][c]
                    rope(qn, S["cos"][:, :N], S["sin"][:, :N], [S["cos"], S["sin"]], qt[:, :N], qt, N, False)
                else:
                    qt = S["QS"][c]
                    rope(qn, rcs_t[:, 0:1], rcs_t[:, 1:2], [rcs_t], qt[:, :N], qt, N, True)
                qts[c] = qt

            def mchain(c):
                pq = proj(slot, 1024, 768 + c * 128, hn, N)
                qn = K.pool("fa", 4, [128, 516], F32)
                yield from headnorm_g(pq, cc(f"mqg{l}"), N, qn)
                qns[c] = qn

            interleave([qchain(c) for c in range(6)] + [mchain(c) for c in range(2)], 2)
            return qts, qns

        def swa_prompt(l, qT, main, S, ci):
            kT = S["kT"]; Vt = S["Vtok"]
            interleave([swa_head(l, qT, main, S, ci, n) for n in range(12)], 2)

        def swa_head(l, qT, main, S, ci, n):
            kT = S["kT"]; Vt = S["Vtok"]
            if True:
                kv = n // 3; o = (n % 2) * 64; qc = n // 2
                PM = []
                for jb in range(5):
                    j = jb - 1
                    if j == -1:
                        if ci == 0:
                            PM.append(None)
                            continue
                        q0, nn, m0 = 0, 128, 128
                    elif j == 3:
                        q0, nn, m0 = 384, 128, 0
                    else:
                        q0, nn, m0 = j * 128, 256, 0
                    sT = K.ps()
                    op("pe", lambda: nc.tensor.matmul(sT[:, :nn], kT[kv][o:o + 64, jb * 128:(jb + 1) * 128], qT[qc][o:o + 64, q0:q0 + nn], start=True, stop=True), [kT[kv], qT[qc]], [sT])
                    pT = bp()
                    op("act", lambda: nc.scalar.activation(out=pT[:, :nn], in_=sT[:, :nn], func=AF.Exp, scale=0.125), [sT], [pT])
                    pm = K.pool("pm", 12, [128, 256], BF16)
                    if (jb + n) % 2 == 0:
                        op("pool", lambda: nc.gpsimd.tensor_tensor(out=pm[:, :nn], in0=pT[:, :nn], in1=mask_bf[:, m0:m0 + nn], op=ALU.mult), [pT, mask_bf], [pm])
                    else:
                        op("dve", lambda: nc.vector.tensor_tensor(out=pm[:, :nn], in0=pT[:, :nn], in1=mask_bf[:, m0:m0 + nn], op=ALU.mult), [pT, mask_bf], [pm])
                    PM.append(pm)
                yield
                fill(5)
                ops_ = K.ps(); dps = K.ps()
                vs = 64 + kv * 64 - o
                for which, lhs_of in ((ops_, lambda blk: Vt[blk][:, vs:vs + 128]), (dps, lambda blk: ones_bf[:])):
                    for i in range(4):
                        parts = []
                        if PM[i] is not None:
                            parts.append((i, PM[i][:, 0:128] if i == 0 else PM[i][:, 128:256], PM[i]))
                        parts.append((i + 1, PM[i + 1][:, 0:128], PM[i + 1]))
                        for pi, (blk, rhs, pmt) in enumerate(parts):
                            rd_t = [pmt] + ([Vt[blk]] if which is ops_ else [ones_bf])
                            op("pe", lambda: nc.tensor.matmul(which[:, i * 128:(i + 1) * 128], lhs_of(blk), rhs, start=(pi == 0), stop=(pi == len(parts) - 1)), rd_t, [which])
                yield
                rd = fp()
                op("act", lambda: nc.scalar.activation(out=rd[o:o + 64, :TC], in_=dps[o:o + 64, :TC], func=AF.Ln, bias=esink[o:o + 64, (l - 2) * 12 + n:(l - 2) * 12 + n + 1]), [dps, esink], [rd])
                op("act", lambda: nc.scalar.activation(out=rd[o:o + 64, :TC], in_=rd[o:o + 64, :TC], func=AF.Exp, scale=-1.0), [rd], [rd])
                op("dve", lambda: nc.vector.tensor_tensor(out=main[qc][o:o + 64, :TC], in0=ops_[o:o + 64, :TC], in1=rd[o:o + 64, :TC], op=ALU.mult), [ops_, rd], [main[qc]])

        def mem_prologue(S, es2):
            MIN = K.sb("memin", [128, 2 * D], F32, es2)
            rowm = K.sb("rowm", [128, 1024], F32, es2)
            K.dma("sp", rowm[:], rowc[0:1024].partition_broadcast(128), writes=[rowm])
            for mb in range(2):
                K.dma("sp", MIN[:, mb * D:(mb + 1) * D], memp[mb * 128:(mb + 1) * 128, :], writes=[MIN])
            MT = [K.sb(f"memT{c}", [128, 256], F32, es2) for c in range(8)]
            for c in range(8):
                pt = K.ps()
                for mb in range(2):
                    transpose_to(pt, mb * 128, MIN[:, mb * D + c * 128: mb * D + (c + 1) * 128], MIN, 128)
                op("act", lambda: nc.scalar.copy(out=MT[c][:, :], in_=pt[:, 0:256]), [pt], [MT[c]])
            ss = K.ps()
            for c in range(8):
                sq = bp()
                op("act", lambda: nc.scalar.activation(out=sq[:, :256], in_=MT[c][:, :], func=AF.Square), [MT[c]], [sq])
                op("pe", lambda: nc.tensor.matmul(ss[:, :256], ones_bf[:], sq[:, :256], start=(c == 0), stop=(c == 7)), [sq, ones_bf], [ss])
            rb = K.sb("memrb", [128, 256], F32, es2)
            op("act", lambda: nc.scalar.activation(out=rb[:, :], in_=ss[:, :256], func=AF.Ln, scale=1.0 / D, bias=EPS), [ss], [rb])
            op("act", lambda: nc.scalar.activation(out=rb[:, :], in_=rb[:, :], func=AF.Exp, scale=-0.5), [rb], [rb])
            MN = [K.sb(f"memn{c}", [128, 256], BF16, es2) for c in range(8)]
            WM = K.sb("wmem_f", [128, 4096], F32, es2)
            WMB = K.sb("wmem_b", [128, 4096], BF16, es2)
            KTOK = K.sb("ktok", [128, 256], F32, es2)
            VTOK = K.sb("vtok", [128, 256], F32, es2)
            KSQ = K.sb("ksq", [128, 256], F32, es2)
            KR = K.sb("kr", [128, 4], F32, es2)
            for l in range(4):
                K.dma("sp", WM[:, :], wmem[l], writes=[WM])
                op("pool", lambda: nc.gpsimd.tensor_copy(out=WMB[:, :], in_=WM[:, :]), [WM], [WMB])
                for c in range(8):
                    op("dve", lambda: nc.vector.scalar_tensor_tensor(out=MN[c][:, :], in0=MT[c][:, :], scalar=cc(f"memg{l}", c), in1=rb[:, :], op0=ALU.mult, op1=ALU.mult), [MT[c], rb, cpk], [MN[c]])
                for mb in range(2):
                    pt = K.ps()
                    for k in range(8):
                        op("pe", lambda: nc.tensor.matmul(pt[:, :512], MN[k][:, mb * 128:(mb + 1) * 128], WMB[:, k * 512:(k + 1) * 512], start=(k == 0), stop=(k == 7)), [MN[k], WMB], [pt])
                    op("act", lambda: nc.scalar.copy(out=VTOK[:, :], in_=pt[:, 256:512]), [pt], [VTOK])
                    op("pool", lambda: nc.gpsimd.tensor_copy(out=S["vm"][l][mb][:, :], in_=VTOK[:, :]), [VTOK], [S["vm"][l][mb]])
                    K.dma("pool", o_pmv[l, mb * 128:(mb + 1) * 128, :], VTOK[:, :], reads=[VTOK])
                    op("act", lambda: nc.scalar.activation(out=KSQ[:, :], in_=pt[:, 0:256], func=AF.Square), [pt], [KSQ])
                    op("dve", lambda: nc.vector.tensor_reduce(out=KR[:, :], in_=KSQ[:, :].rearrange("p (h d) -> p h d", d=64), axis=AX.X, op=ALU.add), [KSQ], [KR])
                    op("act", lambda: nc.scalar.activation(out=KR[:, :], in_=KR[:, :], func=AF.Sqrt, scale=1.0 / HD, bias=EPS), [KR], [KR])
                    op("dve", lambda: nc.vector.reciprocal(out=KR[:, :], in_=KR[:, :]), [KR], [KR])
                    op("dve", lambda: nc.vector.tensor_tensor(out=KTOK[:, :].rearrange("p (h d) -> p h d", d=64), in0=pt[:, 0:256].rearrange("p (h d) -> p h d", d=64), in1=KR[:, :].unsqueeze(2).to_broadcast([128, 4, 64]), op=ALU.mult), [pt, KR], [KTOK])
                    op("dve", lambda: nc.vector.tensor_tensor(out=KTOK[:, :], in0=KTOK[:, :], in1=rowm[:, l * 256:(l + 1) * 256], op=ALU.mult), [KTOK, rowm], [KTOK])
                    K.dma("pool", o_pmk[l, mb * 128:(mb + 1) * 128, :], KTOK[:, :], reads=[KTOK])
                    for c in range(2):
                        p2 = K.ps()
                        transpose_to(p2, 0, KTOK[:, c * 128:(c + 1) * 128], KTOK, 128)
                        op("act", lambda: nc.scalar.copy(out=S["kmT"][l][c][:, mb * 128:(mb + 1) * 128], in_=p2[:, 0:128]), [p2], [S["kmT"][l][c]])

        def sample_pass():
            add_pass()
            with contextlib.ExitStack() as es2:
                S = {}
                N = NS
                X = [K.sb(f"sx{c}", [128, NS], F32, es2) for c in range(8)]
                hn = [K.sb(f"shn{c}", [128, NS], BF16, es2) for c in range(8)]
                main = [K.sb(f"smain{c}", [128, NS], BF16, es2) for c in range(8)]
                act_t = [K.sb(f"sact{c}", [128, NS], BF16, es2) for c in range(24)]
                S["BUFA"] = K.sb("bufa", [128, 8192], F32, es2)
                S["BUFT"] = K.sb("buft", [128, 4096], F32, es2)
                S["QREP"] = K.sb("qrep", [128, 768], F32, es2)
                S["SC"] = K.sb("sc", [128, 192], F32, es2); S["PP"] = K.sb("pp", [128, 192], F32, es2)
                S["DENP"] = K.sb("denp", [128, 12], F32, es2); S["OPART"] = K.sb("opart", [128, 768], F32, es2)
                S["OT"] = K.sb("ot", [16, 768], F32, es2); S["DT"] = K.sb("dt", [16, 12], F32, es2)
                S["RST"] = [K.sb(f"rst{l}", [128, 18 * 16], F32, es2) for l in range(2)]
                S["H0"] = [K.sb(f"h0{l}", [128, 6 * 16], F32, es2) for l in range(2)]
                S["HNEW"] = [K.sb(f"hnew{l}", [128, 6 * 16], F32, es2) for l in range(2)]
                S["XRN"] = [K.sb(f"xrn{l}", [128, 6 * 16], F32, es2) for l in range(2)]
                S["FST"] = K.sb("fst", [128, 96 * 16], F32, es2)
                S["UNEW"] = K.sb("unew", [128, 48 * 16], F32, es2)
                S["KNEW"] = K.sb("knew", [128, 4 * 16], F32, es2)
                S["VNEW"] = K.sb("vnew", [16, 256], F32, es2)
                S["QS"] = [K.sb(f"qs{c}", [128, NS], F32, es2) for c in range(6)]
                STG = K.sb("stg", [16, DFF], F32, es2)
                TOK = STG
                KTOKS = K.sb("ktoks", [16, 256], F32, es2)
                QTOK = K.sb("qtok", [16, 768], F32, es2)
                ENEW = K.sb("enew", [16, 12], F32, es2)
                ETMP = S["BUFT"]

                def load_fm(dst, src_ap, ncols_total):
                    nchk = ncols_total // 128
                    for c0 in range(0, nchk, 32):
                        cn = min(32, nchk - c0)
                        pt = K.ps()
                        for c in range(cn):
                            transpose_to(pt, c * 16, src_ap(c0 + c), src_ap.tile, 16)
                        op("act", lambda: nc.scalar.copy(out=dst[:, c0 * 16:(c0 + cn) * 16], in_=pt[:, 0:cn * 16]), [pt], [dst])

                class Src:
                    def __init__(self, tile, base=0):
                        self.tile = tile
                        self.base = base

                    def __call__(self, c):
                        return self.tile[0:16, self.base + c * 128:self.base + (c + 1) * 128]

                def store_tm(dst_ap_fn, src_tile, nchk, stage, cbase=0):
                    for c0 in range(0, nchk, 4):
                        cn = min(4, nchk - c0)
                        pt = K.ps()
                        for c in range(cn):
                            transpose_to(pt, c * 128, src_tile[:, (cbase + c0 + c) * 16:(cbase + c0 + c + 1) * 16], src_tile, 128)
                        op("act", lambda: nc.scalar.copy(out=stage[0:16, c0 * 128:(c0 + cn) * 128], in_=pt[0:16, 0:cn * 128]), [pt], [stage])
                    K.dma("pool", dst_ap_fn, stage[0:16, 0:nchk * 128], reads=[stage])

                K.dma("sp", STG[0:16, 0:D], xs[:, :], writes=[STG])
                XS = K.sb("xs_fm", [128, 8 * 16], F32, es2)
                load_fm(XS, Src(STG), D)
                for c in range(8):
                    op("pool", lambda: nc.gpsimd.tensor_copy(out=X[c][:, :], in_=XS[:, c * 16:(c + 1) * 16]), [XS], [X[c]])
                for l in range(2):
                    K.dma("sp", TOK[0:16, 0:3 * DRNN], st_rc[l], writes=[TOK])
                    load_fm(S["RST"][l], Src(TOK), 3 * DRNN)
                    K.dma("sp", TOK[0:16, 0:DRNN], st_h[l], writes=[TOK])
                    load_fm(S["H0"][l], Src(TOK), DRNN)
                    pz = K.pseudo(f"src_copy{l}")
                    K.dma("pool", o_src[l, :, 0:2 * DRNN], st_rc[l, :, DRNN:3 * DRNN], semtile=pz)
                for l in range(4):
                    pz = K.pseudo(f"sfc_copy{l}")
                    K.dma("pool", o_sfc[l, :, 0:2 * DFF], st_fc[l, :, 2 * DFF:4 * DFF], semtile=pz)

                for l in range(4):
                    rmsnorm(X, f"gmix{l}", hn, N)
                    if l < 2:
                        qns = mixer_a(l, X, hn, main, N, True, S)
                    else:
                        qts, qns = mixer_b_q(l, hn, N, True, S)
                    sample_attn(c_mk[l], c_mv[l], 32, qns, 4, 1, S, main, 6)
                    if l >= 2:
                        pt = K.ps(); pt2 = K.ps()
                        for c in range(6):
                            transpose_to(pt if c < 4 else pt2, (c % 4) * 128, qts[c][:, 0:16], qts[c], 128)
                        op("act", lambda: nc.scalar.copy(out=QTOK[0:16, 0:512], in_=pt[0:16, 0:512]), [pt], [QTOK])
                        op("act", lambda: nc.scalar.copy(out=QTOK[0:16, 512:768], in_=pt2[0:16, 0:256]), [pt2], [QTOK])

                        def extra(OT, DT, l=l):
                            qv = QTOK[0:16, :].rearrange("p (k g d) -> p k g d", k=4, g=3)
                            kb_ = KTOKS[0:16, :].rearrange("p (k d) -> p k d", k=4).unsqueeze(2).to_broadcast([16, 4, 3, 64])
                            vb_ = S["VNEW"][0:16, :].rearrange("p (k d) -> p k d", k=4).unsqueeze(2).to_broadcast([16, 4, 3, 64])
                            ev = ETMP[0:16, 0:768].rearrange("p (k g d) -> p k g d", k=4, g=3)
                            op("dve", lambda: nc.vector.tensor_tensor(out=ev, in0=qv, in1=kb_, op=ALU.mult), [QTOK, KTOKS], [ETMP])
                            op("dve", lambda: nc.vector.tensor_reduce(out=ENEW[0:16, :], in_=ETMP[0:16, 0:768].rearrange("p (n d) -> p n d", d=64), axis=AX.X, op=ALU.add), [ETMP], [ENEW])
                            op("act", lambda: nc.scalar.activation(out=ENEW[0:16, :], in_=ENEW[0:16, :], func=AF.Exp, scale=0.125), [ENEW], [ENEW])
                            op("dve", lambda: nc.vector.tensor_tensor(out=ev, in0=ENEW[0:16, :].rearrange("p (k g) -> p k g", g=3).unsqueeze(3).to_broadcast([16, 4, 3, 64]), in1=vb_, op=ALU.mult), [ENEW, S["VNEW"]], [ETMP])
                            op("dve", lambda: nc.vector.tensor_tensor(out=OT[0:16, :], in0=OT[0:16, :], in1=ETMP[0:16, 0:768], op=ALU.add), [OT, ETMP], [OT])
                            op("dve", lambda: nc.vector.tensor_tensor(out=DT[0:16, :], in0=DT[0:16, :], in1=ENEW[0:16, :], op=ALU.add), [DT, ENEW], [DT])
                            op("dve", lambda: nc.vector.tensor_tensor(out=DT[0:16, :], in0=DT[0:16, :], in1=esink[0:16, (l - 2) * 12:(l - 2) * 12 + 12], op=ALU.add), [DT, esink], [DT])

                        sample_attn(c_sk[:, :], c_sv[:, :], 16, qts, 12, 3, S, main, 0, extra)
                    out_proj(l, X, main, N)
                    rmsnorm(X, f"gffn{l}", hn, N)
                    for jq in range(4):
                        K.dma("sp", STG[0:16, :], st_fc[l, :, jq * DFF:(jq + 1) * DFF], writes=[STG])
                        pt = K.ps()
                        for c in range(24):
                            transpose_to(pt, c * 16, STG[0:16, c * 128:(c + 1) * 128], STG, 16)
                        op("act", lambda: nc.scalar.copy(out=S["FST"][:, jq * 24 * 16:(jq + 1) * 24 * 16], in_=pt[:, 0:24 * 16]), [pt], [S["FST"]])
                    ffn(l, X, hn, act_t, N, True, S)
                    for hf in range(2):
                        store_tm(o_sfc[l, :, 2 * DFF + hf * DFF:2 * DFF + (hf + 1) * DFF], S["UNEW"], 24, STG, hf * 24)
                    if l < 2:
                        store_tm(o_sh[l], S["HNEW"][l], 6, TOK)
                        store_tm(o_src[l, :, 2 * DRNN:3 * DRNN], S["XRN"][l], 6, TOK)
                    if l == 1:
                        shared_kv(X, hn, N, True, S, 0)
                        pt = K.ps()
                        for kv in range(4):
                            transpose_to(pt, kv * 128, S["KNEW"][:, kv * 16:(kv + 1) * 16], S["KNEW"], 128)
                        op("act", lambda: nc.scalar.copy(out=KTOKS[0:16, :].rearrange("p (k d) -> p k d", d=64), in_=pt[0:16, 0:512].rearrange("p (k e) -> p k e", e=128)[:, :, 0:64]), [pt], [KTOKS])
                        K.dma("pool", o_sk[:, :], KTOKS[0:16, :], reads=[KTOKS])
                        K.dma("pool", o_sv[:, :], S["VNEW"][0:16, :], reads=[S["VNEW"]])
                XO = K.sb("xs_out", [128, 8 * 16], F32, es2)
                for c in range(8):
                    op("pool", lambda: nc.gpsimd.tensor_copy(out=XO[:, c * 16:(c + 1) * 16], in_=X[c][:, :]), [X[c]], [XO])
                store_tm(y_s[:, :], XO, 8, STG)
                K.barrier()

        def prompt_pass():
            with contextlib.ExitStack() as es2:
                S = {}
                S["kmT"] = [[K.sb(f"kmT{l}_{c}", [128, 256], BF16, es2) for c in range(2)] for l in range(4)]
                S["vm"] = [[K.sb(f"vm{l}_{m}", [128, 256], BF16, es2) for m in range(2)] for l in range(4)]
                with contextlib.ExitStack() as es3:
                    mem_prologue(S, es3)
                    K.barrier()
                N = TC
                X = [K.sb(f"x{c}", [128, TC], F32, es2) for c in range(8)]
                hn = [K.sb(f"hn{c}", [128, TC], BF16, es2) for c in range(8)]
                main = [K.sb(f"main{c}", [128, TC], BF16, es2) for c in range(8)]
                act_t = [K.sb(f"act{c}", [128, TC], BF16, es2) for c in range(24)]
                S["HIST"] = [K.sb(f"hist{l}", [128, 96], F32, es2) for l in range(4)]
                S["RH"] = [K.sb(f"rh{l}", [128, 18], F32, es2) for l in range(2)]
                S["HS"] = [K.sb(f"hs{l}", [128, 6], F32, es2) for l in range(2)]
                S["kT"] = [K.sb(f"kT{kv}", [128, 640], BF16, es2) for kv in range(4)]
                S["Vtok"] = [K.sb(f"vt{i}", [128, 384], BF16, es2) for i in range(5)]
                S["qT"] = act_t[0:6]
                S["cos"] = K.sb("cos", [128, TC], F32, es2); S["sin"] = K.sb("sin", [128, TC], F32, es2)
                S["VF"] = K.sb("vf", [128, 256], F32, es2)
                S["KF"] = K.sb("kf", [128, 4 * 128], F32, es2)
                XIN = K.sb("xin", [128, 2 * D], F32, es2)
                for l in range(4):
                    op("pool", lambda: nc.gpsimd.memset(S["HIST"][l][:, :], 0.0), [], [S["HIST"][l]])
                for l in range(2):
                    op("pool", lambda: nc.gpsimd.memset(S["RH"][l][:, :], 0.0), [], [S["RH"][l]])
                    op("pool", lambda: nc.gpsimd.memset(S["HS"][l][:, :], 0.0), [], [S["HS"][l]])
                for i in range(5):
                    op("pool", lambda: nc.gpsimd.memset(S["Vtok"][i][:, :], 0.0), [], [S["Vtok"][i]])

                def load_x(ci, hf):
                    for tb in range(2):
                        K.dma("sp", XIN[:, tb * D:(tb + 1) * D], xp[ci * TC + (hf * 2 + tb) * 128: ci * TC + (hf * 2 + tb + 1) * 128, :], writes=[XIN])

                load_x(0, 0)
                for ci in range(NR + NF):
                    reduced = ci < NR
                    fi = ci - NR
                    add_pass(names_red if reduced else None)
                    if ci == NR and NR > 0:
                        for tl in (S["HS"][0], S["HS"][1], S["RH"][0], S["RH"][1], S["HIST"][0]):
                            op("pool", lambda: nc.gpsimd.tensor_scalar(out=tl[:, :], in0=tl[:, :], scalar1=flag_t[:, 0:1], scalar2=None, op0=ALU.mult), [tl, flag_t], [tl])
                    xh = []
                    for tb in range(2):
                        for fh in range(2):
                            t_ = fp()
                            K.dma("sp", t_[:, 0:512], xp[ci * TC + (2 + tb) * 128: ci * TC + (3 + tb) * 128, fh * 512:(fh + 1) * 512], writes=[t_])
                            xh.append(t_)
                    for c in range(8):
                        pt = K.ps()
                        for tb in range(2):
                            transpose_to(pt, tb * 128, XIN[:, tb * D + c * 128: tb * D + (c + 1) * 128], XIN, 128)
                        for tb in range(2):
                            t_ = xh[tb * 2 + c // 4]
                            transpose_to(pt, (2 + tb) * 128, t_[:, (c % 4) * 128:(c % 4 + 1) * 128], t_, 128)
                        op("act", lambda: nc.scalar.copy(out=X[c][:, :], in_=pt[:, :]), [pt], [X[c]])
                    if ci + 1 < NR + NF:
                        load_x(ci + 1, 0)
                    for l in range(4):
                        if reduced and l == 2:
                            break
                        rmsnorm(X, f"gmix{l}", hn, N)
                        if reduced and l == 1:
                            mixer_a(l, X, hn, main, N, False, S, reduced=True)
                            continue
                        if l < 2:
                            qns = mixer_a(l, X, hn, main, N, False, S)
                        else:
                            qts, qns = mixer_b_q(l, hn, N, False, S)
                        mem_attn_prompt(l, qns, main, N, S)
                        if l >= 2:
                            swa_prompt(l, qts, main, S, fi)
                        out_proj(l, X, main, N)
                        rmsnorm(X, f"gffn{l}", hn, N)
                        ffn(l, X, hn, act_t, N, False, S)
                        if l == 1:
                            K.dma("sp", S["cos"][:, :], rcos[:, fi * TC:(fi + 1) * TC], writes=[S["cos"]])
                            K.dma("sp", S["sin"][:, :], rsin[:, fi * TC:(fi + 1) * TC], writes=[S["sin"]])
                            shared_kv(X, hn, N, False, S, fi)
                    if reduced:
                        continue
                    for tb in range(4):
                        for hlf in range(2):
                            pt = K.ps()
                            for c4 in range(4):
                                c = hlf * 4 + c4
                                transpose_to(pt, c4 * 128, X[c][:, tb * 128:(tb + 1) * 128], X[c], 128)
                            yo = fp()
                            op("act", lambda: nc.scalar.copy(out=yo[:, 0:512], in_=pt[:, :]), [pt], [yo])
                            K.dma("pool", y_p[fi * TC + tb * 128: fi * TC + (tb + 1) * 128, hlf * 512:(hlf + 1) * 512], yo[:, 0:512], reads=[yo])
                FIN = K.sb("fin", [128, 128], F32, es2)
                for l in range(4):
                    pt = K.ps()
                    transpose_to(pt, 0, S["HIST"][l][:, 0:96], S["HIST"][l], 128)
                    op("act", lambda: nc.scalar.copy(out=FIN[0:96, :], in_=pt[0:96, 0:128]), [pt], [FIN])
                    for j in range(2):
                        K.dma("pool", o_pfc[l, j, :].rearrange("(c f) -> c f", f=128), FIN[j * 48:(j + 1) * 48, :], reads=[FIN])
                for l in range(2):
                    pt = K.ps()
                    transpose_to(pt, 0, S["RH"][l][:, 0:18], S["RH"][l], 128)
                    op("act", lambda: nc.scalar.copy(out=FIN[0:18, :], in_=pt[0:18, 0:128]), [pt], [FIN])
                    for j in range(3):
                        K.dma("pool", o_prc[l, j, :].rearrange("(c f) -> c f", f=128), FIN[j * 6:(j + 1) * 6, :], reads=[FIN])
                    pt = K.ps()
                    transpose_to(pt, 0, S["HS"][l][:, 0:6], S["HS"][l], 128)
                    op("act", lambda: nc.scalar.copy(out=FIN[0:6, :], in_=pt[0:6, 0:128]), [pt], [FIN])
                    K.dma("pool", o_ph[l, :].rearrange("(c f) -> c f", f=128), FIN[0:6, :], reads=[FIN])
                K.dma("pool", o_pv[:, :], S["VF"][:, :], reads=[S["VF"]])
                KO = K.sb("ko", [128, 256], F32, es2)
                pt = K.ps()
                for kv in range(4):
                    transpose_to(pt, kv * 128, S["KF"][:, kv * 128:(kv + 1) * 128], S["KF"], 128)
                op("act", lambda: nc.scalar.copy(out=KO[:, :].rearrange("p (k d) -> p k d", d=64), in_=pt[:, 0:512].rearrange("p (k e) -> p k e", e=128)[:, :, 0:64]), [pt], [KO])
                K.dma("pool", o_pk[:, :], KO[:, :], reads=[KO])
                K.barrier()

        prompt_pass()
        sample_pass()
        K.finish()
    return nc


build.cidx = None


def kernel(**inp):
    inp = {k: np.asarray(v) for k, v in inp.items()}
    B, SEQ, _ = inp["x_prompt"].shape
    DB = inp["x_sample"].shape[0]
    ncores = DB // NS
    wpack, _ = make_wpack(inp)
    cpk, cidx = make_cpk(inp)
    cpk_full = np.zeros((128, 1000), np.float32)
    cpk_full[:, :cpk.shape[1]] = cpk
    build.cidx = cidx
    per = ncores // B
    split = (per == 2) and (SEQ // TC) >= 4
    NCH = SEQ // TC
    NR = NCH // 2 - 1 if split else 0
    NF = NCH // 2 + 1 if split else NCH
    nc = build(SEQ, split)
    cm = const_mats()
    rope_by_start = {}
    for st in ((0, NR * TC) if split else (0,)):
        rope_by_start[st] = rope_tables(np.arange(st, st + NF * TC))
    cs, sn = rope_tables(np.array([PAST]))
    rcs_s = np.ascontiguousarray(np.concatenate([cs, sn], axis=1))
    rowc = np.concatenate([np.tile(inp["mem_k_norm_g"][l], 4) for l in range(4)] + [inp["sinks"].reshape(-1)]).astype(np.float32)
    wmem = np.stack([_kpack(inp["w_mem_kv"][l]) for l in range(4)])
    in_maps = []
    for c in range(ncores):
        b = c * B // ncores
        s0 = c * NS
        sl = slice(s0, s0 + NS)
        m = dict(
            xp=(np.ascontiguousarray(inp["x_prompt"][b]) if (not split or c % 2 == 1) else
                np.ascontiguousarray(np.concatenate([inp["x_prompt"][b][:NR * TC], inp["x_prompt"][b][:NF * TC]], axis=0))),
            flag=np.full((128, 1), 1.0 if (split and c % 2 == 1) else 0.0, np.float32),
            xs=np.ascontiguousarray(inp["x_sample"][sl, 0, :]),
            memp=np.ascontiguousarray(inp["mem_prompt"][b]),
            st_h=np.ascontiguousarray(inp["state_rglru_h"][:, sl]),
            st_rc=np.ascontiguousarray(inp["state_rglru_conv"][:, sl].reshape(2, NS, 3 * DRNN)),
            st_fc=np.ascontiguousarray(inp["state_ffn_conv"][:, sl].reshape(4, NS, 4 * DFF)),
            c_sk=np.ascontiguousarray(inp["cache_swa_k"][sl].reshape(NS * 8, 16 * 256)),
            c_sv=np.ascontiguousarray(inp["cache_swa_v"][sl].reshape(NS * 8, 16 * 256)),
            c_mk=np.ascontiguousarray(inp["cache_mem_k"][:, sl].reshape(4, NS * 8, 32 * 256)),
            c_mv=np.ascontiguousarray(inp["cache_mem_v"][:, sl].reshape(4, NS * 8, 32 * 256)),
            wpack=wpack, wmem=wmem, cpk=cpk_full, rcos=rope_by_start[NR * TC if (split and c % 2 == 1) else 0][0], rsin=rope_by_start[NR * TC if (split and c % 2 == 1) else 0][1], rcs_s=rcs_s, rowc=rowc,
            ident=cm["ident"], rotm=cm["rotm"], sel=cm["sel"], ones_bf=cm["ones_bf"], blk_bf=cm["blk_bf"], mask_bf=cm["mask_bf"],
        )
        in_maps.append(m)
    res = run_bass_kernel_spmd(nc, in_maps, core_ids=list(range(ncores)))
    R = res.results
    own = [b * per + (1 if split else 0) for b in range(B)]
    f = np.float32
    if split:
        h2 = SEQ // 2
        y_prompt = np.stack([np.concatenate([R[b * per]["y_p"][:h2], R[b * per + 1]["y_p"][h2 - NR * TC:]], axis=0) for b in range(B)]).astype(f)
    else:
        y_prompt = np.stack([R[c]["y_p"] for c in own]).astype(f)
    y_sample = np.concatenate([R[c]["y_s"] for c in range(ncores)])[:, None, :].astype(f)
    p_h = np.stack([R[c]["o_ph"] for c in own], axis=1).astype(f)
    p_rc = np.stack([R[c]["o_prc"] for c in own], axis=1).astype(f)
    p_fc = np.stack([R[c]["o_pfc"] for c in own], axis=1).astype(f)
    p_k = np.stack([R[c]["o_pk"].reshape(128, 4, 64) for c in own]).astype(f)
    p_v = np.stack([R[c]["o_pv"].reshape(128, 4, 64) for c in own]).astype(f)
    p_mk = np.stack([R[c]["o_pmk"].reshape(4, NMEM, 4, 64) for c in own], axis=1).astype(f)
    p_mv = np.stack([R[c]["o_pmv"].reshape(4, NMEM, 4, 64) for c in own], axis=1).astype(f)
    s_h = np.concatenate([R[c]["o_sh"] for c in range(ncores)], axis=1).astype(f)
    s_rc = np.concatenate([R[c]["o_src"].reshape(2, NS, 3, DRNN) for c in range(ncores)], axis=1).astype(f)
    s_fc = np.concatenate([R[c]["o_sfc"].reshape(4, NS, 2, 2 * DFF) for c in range(ncores)], axis=1).astype(f)
    s_k = np.concatenate([R[c]["o_sk"].reshape(NS, 1, 4, 64) for c in range(ncores)]).astype(f)
    s_v = np.concatenate([R[c]["o_sv"].reshape(NS, 1, 4, 64) for c in range(ncores)]).astype(f)
    return (y_prompt, y_sample, p_h, p_rc, p_fc, p_k, p_v, p_mk, p_mv, s_h, s_rc, s_fc, s_k, s_v)
```
